# Optimizing a Trainium2 kernel written in Bass

```python
import math
import jax, jax.numpy as jnp
from jax import lax
import numpy as np

D_MODEL = 1024
BATCH = 8
SEQ = 8192
DEPTH = 1

HEAD_DIM = 64
N_HEADS_A = D_MODEL // (2 * HEAD_DIM)
N_KV_HEADS_A = N_HEADS_A // 4
N_HEADS_B = D_MODEL // (2 * HEAD_DIM)
WIDTH_A = N_HEADS_A * HEAD_DIM
WIDTH_B = N_HEADS_B * HEAD_DIM
MIX_WIDTH = WIDTH_A + WIDTH_B
D_IN_PROJ = (N_HEADS_A + 2 * N_KV_HEADS_A + 3 * N_HEADS_B) * HEAD_DIM
WINDOW_A = 128
DILATED_BRANCHES = ((128, 1), (512, 4), (2048, 16))
BLOCK = 128
ROPE_THETA = 150000.0
REL_BUCKETS = 32
REL_MAX_DISTANCE = 2048
D_FF = -(-8 * D_MODEL // (3 * 256)) * 256
NORM_EPS = 1e-5

kernel_name = "hymba_swa_sink_dilated_hybrid"


def rmsnorm(x, g):
    xf = x.astype(jnp.float32)
    y = xf * lax.rsqrt(jnp.mean(xf * xf, axis=-1, keepdims=True) + NORM_EPS)
    return (y * g.astype(jnp.float32)).astype(x.dtype)


def rope(t, seq_len):
    half = t.shape[-1] // 2
    inv_freq = ROPE_THETA ** (-jnp.arange(half, dtype=jnp.float32) / half)
    ang = jnp.arange(seq_len, dtype=jnp.float32)[:, None] * inv_freq[None, :]
    cos = jnp.cos(ang)[None, :, None, :].astype(t.dtype)
    sin = jnp.sin(ang)[None, :, None, :].astype(t.dtype)
    t1, t2 = t[..., :half], t[..., half:]
    return jnp.concatenate([t1 * cos - t2 * sin, t1 * sin + t2 * cos], axis=-1)


def t5_bucket(dist):
    max_exact = REL_BUCKETS // 2
    df = jnp.maximum(dist, 1).astype(jnp.float32)
    large = max_exact + (jnp.log(df / max_exact) / math.log(REL_MAX_DISTANCE / max_exact)
                         * (REL_BUCKETS - max_exact)).astype(jnp.int32)
    large = jnp.minimum(large, REL_BUCKETS - 1)
    return jnp.where(dist < max_exact, dist, large)


def banded_attention(q, k, v, n_back, bias=None, sinks=None):
    N, H, L, D = q.shape
    Hkv = k.shape[1]
    G = H // Hkv
    nb = L // BLOCK
    qb = q.reshape(N, Hkv, G, nb, BLOCK, D) * (D ** -0.5)

    def windows(t):
        tb = t.reshape(N, Hkv, nb, BLOCK, D)
        prev = jnp.pad(tb, ((0, 0), (0, 0), (1, 0), (0, 0), (0, 0)))[:, :, :-1]
        return jnp.concatenate([prev, tb], axis=3)

    kw, vw = windows(k), windows(v)
    s = jnp.einsum('nkgbqd,nkbsd->nkgbqs', qb, kw).astype(jnp.float32)
    delta = BLOCK + jnp.arange(BLOCK)[:, None] - jnp.arange(2 * BLOCK)[None, :]
    in_band = (delta >= 0) & (delta <= n_back)
    key_exists = (jnp.arange(nb)[:, None] > 0) | (jnp.arange(2 * BLOCK)[None, :] >= BLOCK)
    mask = in_band[None] & key_exists[:, None, :]
    if bias is not None:
        b = bias.astype(jnp.float32)[:, jnp.clip(delta, 0, n_back)]
        s = s + b.reshape(1, Hkv, G, 1, BLOCK, 2 * BLOCK)
    s = jnp.where(mask, s, -jnp.inf)
    m = jnp.max(s, axis=-1)
    if sinks is not None:
        sk = sinks.astype(jnp.float32).reshape(1, Hkv, G, 1, 1)
        m = jnp.maximum(m, sk)
    p = jnp.exp(s - m[..., None])
    denom = jnp.sum(p, axis=-1)
    if sinks is not None:
        denom = denom + jnp.exp(sk - m)
    o = jnp.einsum('nkgbqs,nkbsd->nkgbqd', p, vw.astype(jnp.float32)) / denom[..., None]
    lse = m + jnp.log(denom)
    return o.reshape(N, H, L, D).astype(q.dtype), lse.reshape(N, H, L)


def sliding_window_sink_gqa(q, k, v, sinks):
    S = q.shape[1]
    q, k = rope(q, S), rope(k, S)
    o, _ = banded_attention(q.transpose(0, 2, 1, 3), k.transpose(0, 2, 1, 3),
                            v.transpose(0, 2, 1, 3), WINDOW_A - 1, sinks=sinks)
    return o.transpose(0, 2, 1, 3)


def dilated_mixture(q, k, v, rel_table):
    B, S, H, D = q.shape
    outs, lses = [], []
    for window, dil in DILATED_BRANCHES:
        n_back = window // dil
        span = dil * BLOCK
        s_pad = -(-S // span) * span
        L = s_pad // dil

        def to_sub(t):
            t = jnp.pad(t, ((0, 0), (0, s_pad - S), (0, 0), (0, 0)))
            return t.reshape(B, L, dil, H, D).transpose(0, 2, 3, 1, 4).reshape(B * dil, H, L, D)

        bias = rel_table[t5_bucket(jnp.arange(n_back + 1) * dil)].T
        o, lse = banded_attention(to_sub(q), to_sub(k), to_sub(v), n_back, bias=bias)
        o = o.reshape(B, dil, H, L, D).transpose(0, 3, 1, 2, 4).reshape(B, s_pad, H, D)[:, :S]
        lse = lse.reshape(B, dil, H, L).transpose(0, 3, 1, 2).reshape(B, s_pad, H)[:, :S]
        outs.append(o)
        lses.append(lse)
    w = jax.nn.softmax(jnp.stack(lses, axis=0), axis=0)
    out = jnp.sum(w[..., None] * jnp.stack(outs, axis=0).astype(jnp.float32), axis=0)
    return out.astype(q.dtype)


def setup_inputs(seed: int = 0) -> dict:
    key = jax.random.key(seed)
    ks = jax.random.split(key, 16)
    f32 = jnp.float32
    nrm = lambda k, shape, scale: jax.random.normal(k, shape, f32) * scale
    return {
        "x": jax.random.normal(ks[0], (BATCH, SEQ, D_MODEL), f32),
        "g_attn": 1.0 + nrm(ks[1], (DEPTH, D_MODEL), 0.01),
        "w_in": nrm(ks[2], (DEPTH, D_MODEL, D_IN_PROJ), D_MODEL ** -0.5),
        "b_in": nrm(ks[3], (DEPTH, D_IN_PROJ), 0.01),
        "sinks": nrm(ks[4], (DEPTH, N_HEADS_A), 0.5),
        "rel_table": nrm(ks[5], (REL_BUCKETS, N_HEADS_B), 0.5),
        "g_out_a": 1.0 + nrm(ks[6], (DEPTH, WIDTH_A), 0.01),
        "g_out_b": 1.0 + nrm(ks[7], (DEPTH, WIDTH_B), 0.01),
        "w_o": nrm(ks[8], (DEPTH, MIX_WIDTH, D_MODEL), MIX_WIDTH ** -0.5),
        "g_ffn": 1.0 + nrm(ks[9], (DEPTH, D_MODEL), 0.01),
        "w_gate": nrm(ks[10], (DEPTH, D_MODEL, D_FF), D_MODEL ** -0.5),
        "w_up": nrm(ks[11], (DEPTH, D_MODEL, D_FF), D_MODEL ** -0.5),
        "w_down": nrm(ks[12], (DEPTH, D_FF, D_MODEL), D_FF ** -0.5),
        "g_final": 1.0 + nrm(ks[13], (D_MODEL,), 0.01),
    }


def reference(x, g_attn, w_in, b_in, sinks, rel_table, g_out_a, g_out_b, w_o,
              g_ffn, w_gate, w_up, w_down, g_final):
    B, S, _ = x.shape
    splits = np.cumsum([WIDTH_A, N_KV_HEADS_A * HEAD_DIM, N_KV_HEADS_A * HEAD_DIM,
                        WIDTH_B, WIDTH_B])
    for l in range(DEPTH):
        h = rmsnorm(x, g_attn[l])
        proj = jnp.einsum('bsd,de->bse', h, w_in[l]) + b_in[l]
        qa, ka, va, qb, kb, vb = jnp.split(proj, splits, axis=-1)
        qa = qa.reshape(B, S, N_HEADS_A, HEAD_DIM)
        ka = ka.reshape(B, S, N_KV_HEADS_A, HEAD_DIM)
        va = va.reshape(B, S, N_KV_HEADS_A, HEAD_DIM)
        qb = qb.reshape(B, S, N_HEADS_B, HEAD_DIM)
        kb = kb.reshape(B, S, N_HEADS_B, HEAD_DIM)
        vb = vb.reshape(B, S, N_HEADS_B, HEAD_DIM)
        oa = sliding_window_sink_gqa(qa, ka, va, sinks[l]).reshape(B, S, WIDTH_A)
        ob = dilated_mixture(qb, kb, vb, rel_table).reshape(B, S, WIDTH_B)
        mixed = jnp.concatenate([rmsnorm(oa, g_out_a[l]), rmsnorm(ob, g_out_b[l])], axis=-1)
        x = x + jnp.einsum('bse,ed->bsd', mixed, w_o[l])
        h = rmsnorm(x, g_ffn[l])
        act = jax.nn.silu(jnp.einsum('bsd,df->bsf', h, w_gate[l])) * jnp.einsum('bsd,df->bsf', h, w_up[l])
        x = x + jnp.einsum('bsf,fd->bsd', act, w_down[l])
    return rmsnorm(x, g_final)
```

```python
import contextlib
import numpy as np
import concourse.bass as bass
import concourse.mybir as mybir
from concourse.bass_utils import run_bass_kernel_spmd

F32 = mybir.dt.float32
BF16 = mybir.dt.bfloat16
AF = mybir.ActivationFunctionType
ALU = mybir.AluOpType

D = 1024
HD = 64
DFF = 2816
EPS = 1e-5
NEG = -30000.0
NCOLW = 2944
A_ORDER = [0, 4, 1, 5, 2, 6, 3, 7]
DILS = (1, 4, 16)


class Tk:
    __slots__ = ("sem", "val")

    def __init__(self, sem, val):
        self.sem = sem
        self.val = val


class Buf:
    __slots__ = ("name", "w", "r")

    def __init__(self, name=""):
        self.name = name
        self.w = None
        self.r = {}


class Eng:
    def __init__(self, name, h, sem, is_pe=False):
        self.name = name
        self.h = h
        self.sem = sem
        self.n = 0
        self.seen = {}
        self.is_pe = is_pe
        self.pending = []

    def wait(self, tk):
        if tk is None:
            return
        if self.is_pe and tk.sem is self.sem:
            return
        k = id(tk.sem)
        if self.seen.get(k, 0) >= tk.val:
            return
        self.h.wait_ge(tk.sem, tk.val)
        self.seen[k] = tk.val

    def deps(self, reads, writes):
        for b in reads:
            self.wait(b.w)
        for b in writes:
            self.wait(b.w)
            for t in list(b.r.values()):
                self.wait(t)

    @staticmethod
    def commit(tk, reads, writes):
        k = id(tk.sem)
        for b in reads:
            b.r[k] = tk
        for b in writes:
            b.w = tk
            b.r = {}

    def run(self, reads, writes, fn, mark=True):
        self.deps(reads, writes)
        ins = fn(self.h)
        if self.is_pe and not mark:
            self.pending.append((reads, writes))
            return None
        self.n += 1
        ins.then_inc(self.sem, 1)
        tk = Tk(self.sem, self.n)
        for (r, w) in self.pending:
            self.commit(tk, r, w)
        self.pending = []
        self.commit(tk, reads, writes)
        return tk


class DSem:
    def __init__(self, sem):
        self.sem = sem
        self.n = 0


class Ctx:
    def __init__(self, nc):
        self.nc = nc
        self.pe = Eng("pe", nc.tensor, nc.alloc_semaphore("c_pe"), is_pe=True)
        self.act = Eng("act", nc.scalar, nc.alloc_semaphore("c_act"))
        self.dve = Eng("dve", nc.vector, nc.alloc_semaphore("c_dve"))
        self.pool = Eng("pool", nc.gpsimd, nc.alloc_semaphore("c_pool"))
        self.sp = Eng("sp", nc.sync, None)
        self.nsem = 0
        self.dsems = []

    def dsem(self):
        self.nsem += 1
        d = DSem(self.nc.alloc_semaphore("d%d" % self.nsem))
        self.dsems.append(d)
        return d

    def dma(self, out, in_, reads, writes, ds):
        q = self.sp
        q.deps(reads, writes)
        ins = q.h.dma_start(out=out, in_=in_)
        ds.n += 16
        ins.then_inc(ds.sem, 16)
        tk = Tk(ds.sem, ds.n)
        Eng.commit(tk, reads, writes)
        return tk

    def barrier(self):
        tks = []
        for e in (self.pe, self.act, self.dve, self.pool):
            assert not e.pending
            if e.n:
                tks.append(Tk(e.sem, e.n))
        for d in self.dsems:
            if d.n:
                tks.append(Tk(d.sem, d.n))
        for e in (self.pe, self.act, self.dve, self.pool, self.sp):
            for t in tks:
                if e.is_pe and t.sem is e.sem:
                    continue
                k = id(t.sem)
                if e.seen.get(k, 0) >= t.val:
                    continue
                e.h.wait_ge(t.sem, t.val)
                e.seen[k] = t.val


def build(S=8192, dbg=False, phases="ABC", stop=None):
    NT = S // 512
    NTT = S // 128
    NSP = S // 2048
    NR = S // 256
    nc = bass.Bass("TRN2", target_bir_lowering=False)
    cx = Ctx(nc)
    PE, ACT, DVE, POOL = cx.pe, cx.act, cx.dve, cx.pool

    def din(name, shape, dt=F32):
        return nc.dram_tensor(name, list(shape), dt, kind="ExternalInput").ap()

    x = din("x", [S, D])
    w_in_p = din("w_in_p", [D, NCOLW])
    b_fm = din("b_fm", [128, 18])
    b_v = din("b_v", [128, 640])
    gat = din("gat", [128, 8])
    cs = din("cs", [2, 128, S])
    maskb = din("maskb", [128, 25, 256])
    identf = din("identf", [128, 128])
    sinks_bc = din("sinks_bc", [128, 8])
    w_o_p = din("w_o_p", [D, D])
    gout = din("gout", [128, 8])
    gffn = din("gffn", [128, 8])
    w_gate = din("w_gate", [D, DFF])
    w_up = din("w_up", [D, DFF])
    w_down = din("w_down", [DFF, D])
    gfin_bc = din("gfin_bc", [128, D])
    okind = "ExternalOutput"
    y = nc.dram_tensor("y", [S, D], F32, kind=okind).ap()
    skind = "ExternalOutput" if dbg else "Internal"
    qkt = nc.dram_tensor("qkt", [13, 128, S], BF16, kind=skind).ap()
    vsc = nc.dram_tensor("vsc", [S, 650], BF16, kind=skind).ap()
    usc = nc.dram_tensor("usc", [S, 16 * 65], F32, kind=skind).ap()
    wbf_o = nc.dram_tensor("wbf_o", [D, D], BF16).ap()
    wbf_g = nc.dram_tensor("wbf_g", [D, DFF], BF16).ap()
    wbf_u = nc.dram_tensor("wbf_u", [D, DFF], BF16).ap()
    wbf_d = nc.dram_tensor("wbf_d", [DFF, D], BF16).ap()

    def phase_a():
        with contextlib.ExitStack() as ph:
            def sb(name, shape, dt):
                return ph.enter_context(nc.sbuf_tensor(name, list(shape), dt))

            def ps(name, shape, dt=F32):
                return ph.enter_context(nc.psum_tensor(name, list(shape), dt))

            Wp = sb("Wp", [128, 8, NCOLW], BF16)
            Wp_b = Buf("Wp")
            gat_t = sb("gat_t", [128, 8], F32)
            bfm_t = sb("bfm_t", [128, 18], F32)
            bv_t = sb("bv_t", [128, 640], F32)
            idf_t = sb("idf_t", [128, 128], F32)
            idb_t = sb("idb_t", [128, 128], BF16)
            cst_b = Buf("consts")
            dsc = cx.dsem()
            cx.dma(gat_t[:], gat[:, :], [], [cst_b], dsc)
            cx.dma(bfm_t[:], b_fm[:, :], [], [cst_b], dsc)
            cx.dma(bv_t[:], b_v[:, :], [], [cst_b], dsc)
            cx.dma(idf_t[:], identf[:, :], [], [cst_b], dsc)
            gout_a = sb("gout_a", [128, 8], F32)
            gffn_a = sb("gffn_a", [128, 8], F32)
            cx.dma(gout_a[:], gout[:, :], [], [cst_b], dsc)
            cx.dma(gffn_a[:], gffn[:, :], [], [cst_b], dsc)
            idb_b = Buf("idb")
            DVE.run([cst_b], [idb_b], lambda h: h.tensor_copy(out=idb_t[:], in_=idf_t[:]))
            with contextlib.ExitStack() as prep:
                stg = [prep.enter_context(nc.sbuf_tensor("wstg%d" % i, [128, NCOLW], F32)) for i in range(2)]
                stg_b = [Buf("wstg%d" % i) for i in range(2)]
                stg_s = [cx.dsem() for _ in range(2)]
                for k in range(8):
                    i = k % 2
                    cx.dma(stg[i][:], w_in_p[k * 128:(k + 1) * 128, :], [], [stg_b[i]], stg_s[i])
                    hc = NCOLW // 2
                    DVE.run([stg_b[i], cst_b], [Wp_b],
                            lambda h: h.tensor_scalar(out=Wp[:, k, 0:hc], in0=stg[i][:, 0:hc],
                                                      scalar1=gat_t[:, k:k + 1], scalar2=None, op0=ALU.mult))
                    ACT.run([stg_b[i], cst_b], [Wp_b],
                            lambda h: h.activation(out=Wp[:, k, hc:NCOLW], in_=stg[i][:, hc:NCOLW], func=AF.Copy,
                                                   scale=gat_t[:, k:k + 1]))
                cx.barrier()
            if stop == "prep":
                return
            NXS = 4
            xt = [sb("xt%d" % i, [128, D], F32) for i in range(NXS)]
            xt_b = [Buf() for _ in range(NXS)]
            xt_s = [cx.dsem() for _ in range(NXS)]
            junk = sb("junk", [128, D], BF16)
            st = [sb("st%d" % i, [128, 4], F32) for i in range(4)]
            st_b = [Buf() for _ in range(4)]
            hb = [sb("hb%d" % i, [128, D], BF16) for i in range(2)]
            hb_b = [Buf() for _ in range(2)]
            tp = [ps("tp%d" % i, [128, D], BF16) for i in range(2)]
            tp_b = [Buf() for _ in range(2)]
            hT = [sb("hT%d" % i, [128, 8, 512], BF16) for i in range(2)]
            hT_b = [Buf() for _ in range(2)]
            cst = [sb("cs%d" % i, [128, 2, 512], F32) for i in range(2)]
            cs_b = [Buf() for _ in range(2)]
            cs_s = [cx.dsem() for _ in range(2)]
            NMM = 6
            mm = [ps("mm%d" % i, [128, 512], F32) for i in range(NMM)]
            mm_b = [Buf() for _ in range(NMM)]
            t1 = [sb("t1_%d" % i, [128, 512], F32) for i in range(2)]
            t1_b = [Buf() for _ in range(2)]
            t2 = [sb("t2_%d" % i, [128, 512], F32) for i in range(2)]
            t2_b = [Buf() for _ in range(2)]
            NO = 4
            ob = [sb("ob%d" % i, [128, 512], BF16) for i in range(NO)]
            ob_b = [Buf() for _ in range(NO)]
            ob_s = [cx.dsem() for _ in range(NO)]
            vp = [sb("vp%d" % i, [128, 10, 65], BF16) for i in range(2)]
            vp_b = [Buf() for _ in range(2)]
            vp_s = [cx.dsem() for _ in range(2)]
            for i in range(2):
                POOL.run([], [vp_b[i]], lambda h: h.memset(vp[i][:], 1.0))

            def load_x(t):
                i = t % NXS
                cx.dma(xt[i][:], x[t * 128:(t + 1) * 128, :], [], [xt_b[i]], xt_s[i])

            for t in range(min(NXS, NTT)):
                load_x(t)
            cnt = dict(mm=0, o=0, rope=0)

            def nxt(key, n):
                i = cnt[key] % n
                cnt[key] += 1
                return i

            def front(T):
                Ts = T % 2
                cx.dma(cst[Ts][:, 0, :], cs[0, :, T * 512:(T + 1) * 512], [], [cs_b[Ts]], cs_s[Ts])
                cx.dma(cst[Ts][:, 1, :], cs[1, :, T * 512:(T + 1) * 512], [], [cs_b[Ts]], cs_s[Ts])

                def stage_e(tt):
                    t = 4 * T + tt
                    xi = t % NXS
                    si = t % 4
                    hi = t % 2
                    ACT.run([xt_b[xi]], [st_b[si]],
                            lambda h: h.activation(out=junk[:], in_=xt[xi][:], func=AF.Square,
                                                   accum_out=st[si][:, 0:1]))
                    ACT.run([], [st_b[si]],
                            lambda h: h.activation(out=st[si][:, 1:2], in_=st[si][:, 0:1], func=AF.Sqrt,
                                                   scale=1.0 / D, bias=EPS))
                    DVE.run([], [st_b[si]], lambda h: h.reciprocal(out=st[si][:, 2:3], in_=st[si][:, 1:2]))
                    DVE.run([xt_b[xi], st_b[si]], [hb_b[hi]],
                            lambda h: h.tensor_scalar(out=hb[hi][:], in0=xt[xi][:], scalar1=st[si][:, 2:3],
                                                      scalar2=None, op0=ALU.mult))
                    if t + NXS < NTT:
                        load_x(t + NXS)

                def stage_p(tt):
                    t = 4 * T + tt
                    hi = t % 2
                    for k in range(8):
                        PE.run([hb_b[hi], idb_b], [tp_b[hi]],
                               lambda h: h.transpose(out=tp[hi][:, k * 128:(k + 1) * 128],
                                                     in_=hb[hi][:, k * 128:(k + 1) * 128], identity=idb_t[:]),
                               mark=(k == 7))
                    ACT.run([tp_b[hi]], [hT_b[Ts]],
                            lambda h: h.activation(out=hT[Ts][:, :, tt * 128:(tt + 1) * 128],
                                                   in_=tp[hi][:].rearrange("p (k t) -> p k t", k=8), func=AF.Copy))

                for (kind, tt) in (("e", 0), ("e", 1), ("p", 0), ("e", 2), ("p", 1), ("e", 3), ("p", 2), ("p", 3)):
                    if kind == "e":
                        stage_e(tt)
                    else:
                        stage_p(tt)
                    yield

            def back(T, gen):
                Ts = T % 2

                def mm_fm(col0):
                    i = nxt("mm", NMM)
                    for k in range(8):
                        PE.run([Wp_b, hT_b[Ts]], [mm_b[i]],
                               lambda h: h.matmul(mm[i][:, :], lhsT=Wp[:, k, col0:col0 + 128], rhs=hT[Ts][:, k, :],
                                                  start=(k == 0), stop=(k == 7)),
                               mark=(k == 7))
                    return i

                for c in range(13):
                    o = nxt("o", NO)
                    if c <= 4:
                        rc = 2304 + c * 128
                        bi = 13 + c
                        i0 = mm_fm(c * 128)
                        i1 = mm_fm(rc)
                        r = nxt("rope", 2)
                        DVE.run([mm_b[i0], cs_b[Ts], cst_b], [t1_b[r]],
                                lambda h: h.scalar_tensor_tensor(
                                    out=t1[r][:], in0=mm[i0][:, :], scalar=bfm_t[:, c:c + 1], in1=cst[Ts][:, 0, :],
                                    op0=ALU.add, op1=ALU.mult))
                        DVE.run([mm_b[i1], cs_b[Ts], cst_b], [t2_b[r]],
                                lambda h: h.scalar_tensor_tensor(
                                    out=t2[r][:], in0=mm[i1][:, :], scalar=bfm_t[:, bi:bi + 1], in1=cst[Ts][:, 1, :],
                                    op0=ALU.add, op1=ALU.mult))
                        DVE.run([t1_b[r], t2_b[r]], [ob_b[o]],
                                lambda h: h.tensor_tensor(out=ob[o][:], in0=t1[r][:], in1=t2[r][:], op=ALU.add))
                    elif c <= 8:
                        i0 = mm_fm(c * 128)
                        DVE.run([mm_b[i0], cst_b], [ob_b[o]],
                                lambda h: h.tensor_scalar(out=ob[o][:], in0=mm[i0][:, :], scalar1=bfm_t[:, c:c + 1],
                                                          scalar2=0.125, op0=ALU.add, op1=ALU.mult))
                    else:
                        i0 = mm_fm(c * 128)
                        ACT.run([mm_b[i0], cst_b], [ob_b[o]],
                                lambda h: h.activation(out=ob[o][:], in_=mm[i0][:, :], func=AF.Identity,
                                                       bias=bfm_t[:, c:c + 1], scale=1.0))
                    cx.dma(qkt[c, :, T * 512:(T + 1) * 512], ob[o][:], [ob_b[o]], [], ob_s[o])
                    if c in (2, 6, 10):
                        wjob_step()
                    if gen is not None:
                        next(gen, None)
                for tt in range(4):
                    t = 4 * T + tt
                    vi = t % 2
                    ia = nxt("mm", NMM)
                    ib = nxt("mm", NMM)
                    for k in range(8):
                        PE.run([Wp_b, hT_b[Ts]], [mm_b[ia]],
                               lambda h: h.matmul(mm[ia][:, :], lhsT=hT[Ts][:, k, tt * 128:(tt + 1) * 128],
                                                  rhs=Wp[:, k, 1664:2176], start=(k == 0), stop=(k == 7)),
                               mark=(k == 7))
                    for k in range(8):
                        PE.run([Wp_b, hT_b[Ts]], [mm_b[ib]],
                               lambda h: h.matmul(mm[ib][:, 0:128], lhsT=hT[Ts][:, k, tt * 128:(tt + 1) * 128],
                                                  rhs=Wp[:, k, 2176:2304], start=(k == 0), stop=(k == 7)),
                               mark=(k == 7))
                    DVE.run([mm_b[ia], cst_b], [vp_b[vi]],
                            lambda h: h.tensor_tensor(
                                out=vp[vi][:, 0:8, 0:64], in0=mm[ia][:, :].rearrange("p (h d) -> p h d", h=8),
                                in1=bv_t[:, 0:512].rearrange("p (h d) -> p h d", h=8), op=ALU.add))
                    DVE.run([mm_b[ib], cst_b], [vp_b[vi]],
                            lambda h: h.tensor_tensor(
                                out=vp[vi][:, 8:10, 0:64], in0=mm[ib][:, 0:128].rearrange("p (h d) -> p h d", h=2),
                                in1=bv_t[:, 512:640].rearrange("p (h d) -> p h d", h=2), op=ALU.add))
                    cx.dma(vsc[t * 128:(t + 1) * 128, :], vp[vi][:].rearrange("p h d -> p (h d)"),
                           [vp_b[vi]], [], vp_s[vi])

            wst = [sb("wcs%d" % i, [128, DFF], F32) for i in range(2)]
            wst_b = [Buf() for _ in range(2)]
            wst_s = [cx.dsem() for _ in range(2)]
            wbo = [sb("wcb%d" % i, [128, DFF], BF16) for i in range(2)]
            wbo_b = [Buf() for _ in range(2)]
            wbo_s = [cx.dsem() for _ in range(2)]
            wjobs = []
            for k in range(8):
                wjobs.append((w_o_p[k * 128:(k + 1) * 128, :], D, wbf_o[k * 128:(k + 1) * 128, :], gout_a, k))
            for k in range(8):
                wjobs.append((w_gate[k * 128:(k + 1) * 128, :], DFF, wbf_g[k * 128:(k + 1) * 128, :], gffn_a, k))
            for k in range(8):
                wjobs.append((w_up[k * 128:(k + 1) * 128, :], DFF, wbf_u[k * 128:(k + 1) * 128, :], gffn_a, k))
            for f in range(22):
                wjobs.append((w_down[f * 128:(f + 1) * 128, :], D, wbf_d[f * 128:(f + 1) * 128, :], None, 0))
            wj = [0]

            def wjob_load(ji):
                if ji >= len(wjobs):
                    return
                (src, ncol, dst, gt, k) = wjobs[ji]
                i = ji % 2
                cx.dma(wst[i][:, 0:ncol], src, [], [wst_b[i]], wst_s[i])

            def wjob_step():
                ji = wj[0]
                if ji >= len(wjobs) or "C" not in phases:
                    return
                wj[0] += 1
                if ji == 0:
                    wjob_load(0)
                wjob_load(ji + 1)
                (src, ncol, dst, gt, k) = wjobs[ji]
                i = ji % 2
                hcol = ncol // 2
                if gt is not None:
                    DVE.run([wst_b[i], cst_b], [wbo_b[i]],
                            lambda h: h.tensor_scalar(out=wbo[i][:, 0:hcol], in0=wst[i][:, 0:hcol],
                                                      scalar1=gt[:, k:k + 1], scalar2=None, op0=ALU.mult))
                    ACT.run([wst_b[i], cst_b], [wbo_b[i]],
                            lambda h: h.activation(out=wbo[i][:, hcol:ncol], in_=wst[i][:, hcol:ncol], func=AF.Copy,
                                                   scale=gt[:, k:k + 1]))
                else:
                    DVE.run([wst_b[i]], [wbo_b[i]],
                            lambda h: h.tensor_copy(out=wbo[i][:, 0:hcol], in_=wst[i][:, 0:hcol]))
                    ACT.run([wst_b[i]], [wbo_b[i]],
                            lambda h: h.activation(out=wbo[i][:, hcol:ncol], in_=wst[i][:, hcol:ncol], func=AF.Copy))
                cx.dma(dst, wbo[i][:, 0:ncol], [wbo_b[i]], [], wbo_s[i])

            for _ in front(0):
                pass
            for T in range(NT):
                gen = front(T + 1) if T + 1 < NT else None
                back(T, gen)
                if gen is not None:
                    for _ in gen:
                        pass
            while wj[0] < len(wjobs) and "C" in phases:
                wjob_step()
            cx.barrier()

    def phase_b():
        with contextlib.ExitStack() as ph:
            def sb(name, shape, dt):
                return ph.enter_context(nc.sbuf_tensor(name, list(shape), dt))

            def ps(name, shape, dt=F32):
                return ph.enter_context(nc.psum_tensor(name, list(shape), dt))

            idf_t = sb("b_idf", [128, 128], F32)
            idb_t = sb("b_idb", [128, 128], BF16)
            masks = sb("masks", [128, 12, 256], F32)
            masks_b = Buf("masks")
            masks_s = cx.dsem()
            cst_b = Buf("bconst")
            dsc = cx.dsem()
            cx.dma(idf_t[:], identf[:, :], [], [cst_b], dsc)
            DVE.run([cst_b], [cst_b], lambda h: h.tensor_copy(out=idb_t[:], in_=idf_t[:]))
            qt = [sb("qt%d" % i, [128, 2, 2048], BF16) for i in range(2)]
            qt_b = [Buf() for _ in range(2)]
            qt_s = [cx.dsem() for _ in range(2)]
            qd = [[sb("qd%d_%d" % (j, i), [128, 2, 2048], BF16) for i in range(2)] for j in range(2)]
            qd_b = [[Buf() for _ in range(2)] for _ in range(2)]
            kt = [sb("kt%d" % i, [128, 2, 2048], BF16) for i in range(3)]
            kt_b = [Buf() for _ in range(3)]
            kt_s = [cx.dsem() for _ in range(3)]
            vt = [[sb("vt%d_%d" % (br, i), [128, 16, 260], BF16) for i in range(3)] for br in range(3)]
            vt_b = [[Buf() for _ in range(3)] for _ in range(3)]
            vt_s = [[cx.dsem() for _ in range(3)] for _ in range(3)]
            NPT = 8
            pT = [sb("pT%d" % i, [128, 512], BF16) for i in range(NPT)]
            pT_b = [Buf() for _ in range(NPT)]
            NSS = 6
            ssb = [sb("ssb%d" % i, [128, 512], F32) for i in range(NSS)]
            ssb_b = [Buf() for _ in range(NSS)]
            accsb = [sb("accsb%d" % i, [128, 2048], F32) for i in range(1)]
            accsb_b = [Buf() for _ in range(1)]
            usb = [sb("usb%d" % i, [128, 16, 4, 65], F32) for i in range(1)]
            usb_b = [Buf() for _ in range(1)]
            usb_s = [cx.dsem() for _ in range(1)]
            acc = [ps("acc%d" % i, [128, 512], F32) for i in range(4)]
            acc_b = [Buf() for _ in range(4)]
            sps = [ps("sps%d" % i, [128, 512], F32) for i in range(2)]
            sps_b = [Buf() for _ in range(2)]
            tpu_full = [ps("tpu%d" % i, [128, 512], F32) for i in range(2)]
            tpu = [tf[:, 0:260].rearrange("p (i d) -> p i d", i=4) for tf in tpu_full]
            tpu_b = [Buf() for _ in range(2)]

            cnt = dict(q=0, s=0, p=0, a=0, tp=0, ss=0)

            def nxt(key, n):
                i = cnt[key] % n
                cnt[key] += 1
                return i

            passes = []
            for a in range(2):
                passes.append(dict(grp="A", qch=[2 * a, 2 * a + 1], kch=[4], vc0=0, nvh=2, brs=[0], scale=0.125,
                                   uslot=a))
            for b in range(2):
                passes.append(dict(grp="B", qch=[5 + 2 * b, 6 + 2 * b], kch=[9 + 2 * b, 10 + 2 * b],
                                   vc0=(2 + 4 * b) * 65, nvh=4, brs=[0, 1, 2], scale=1.0, uslot=2 + b))

            def unit_off(dl, u):
                if dl == 1:
                    return u * 128
                if dl == 4:
                    return (u // 4) * 512 + (u % 4)
                return u

            def load_span(pz, s, g):
                par = g % 3
                qs = g % 2
                base = s * 2048
                C = pz["nvh"] * 65
                for i, qc in enumerate(pz["qch"]):
                    cx.dma(qt[qs][:, i, :], qkt[qc, :, base:base + 2048], [], [qt_b[qs]], qt_s[qs])
                for i, kc in enumerate(pz["kch"]):
                    cx.dma(kt[par][:, i, :], qkt[kc, :, base:base + 2048], [], [kt_b[par]], kt_s[par])
                for br in pz["brs"]:
                    dl = DILS[br]
                    src = vsc[base:base + 2048, pz["vc0"]:pz["vc0"] + C]
                    dst = vt[br][par]
                    if dl == 1:
                        for hh in range(2):
                            cx.dma(dst[:, 8 * hh:8 * hh + 8, 0:C],
                                   src[hh * 1024:(hh + 1) * 1024, :].rearrange("(u j) c -> j u c", j=128),
                                   [], [vt_b[br][par]], vt_s[br][par])
                    elif dl == 4:
                        for b4 in range(4):
                            cx.dma(dst[:, 4 * b4:4 * b4 + 4, 0:C],
                                   src[b4 * 512:(b4 + 1) * 512, :].rearrange("(j r) c -> j r c", r=4),
                                   [], [vt_b[br][par]], vt_s[br][par])
                    else:
                        s3 = src.rearrange("(j r) c -> j r c", r=16)
                        for hh in range(2):
                            cx.dma(dst[:, 8 * hh:8 * hh + 8, 0:C], s3[:, 8 * hh:8 * hh + 8, :], [],
                                   [vt_b[br][par]], vt_s[br][par])

            def destride(pz, g):
                qs = g % 2
                if pz["grp"] == "B":
                    for i in range(2):
                        ACT.run([qt_b[qs]], [qd_b[0][qs]],
                                lambda h: h.activation(
                                    out=qd[0][qs][:, i, :].rearrange("p (b r j) -> p b r j", b=4, r=4),
                                    in_=qt[qs][:, i, :].rearrange("p (b j r) -> p b r j", b=4, r=4), func=AF.Copy))
                        ACT.run([qt_b[qs]], [qd_b[1][qs]],
                                lambda h: h.activation(
                                    out=qd[1][qs][:, i, :].rearrange("p (r j) -> p r j", r=16),
                                    in_=qt[qs][:, i, :].rearrange("p (j r) -> p r j", r=16), func=AF.Copy))

            def load_masks(pz):
                if pz["grp"] == "A":
                    cx.dma(masks[:, 0:1, :], maskb[:, 0:1, :], [], [masks_b], masks_s)
                else:
                    b = pz["uslot"] - 2
                    for br in range(3):
                        m0 = 1 + br * 8 + 4 * b
                        cx.dma(masks[:, br * 4:br * 4 + 4, :], maskb[:, m0:m0 + 4, :], [], [masks_b], masks_s)

            def job_info(pz, s, g, dl, units):
                par = g % 3
                ppar_ = (g - 1) % 3
                info = []
                for u in units:
                    if dl == 1:
                        pu = u - 1 if u > 0 else 15
                        ppar = par if u > 0 else ppar_
                        pex = (u > 0) or (s > 0)
                    elif dl == 4:
                        pu = u - 4 if u >= 4 else u + 12
                        ppar = par if u >= 4 else ppar_
                        pex = (u >= 4) or (s > 0)
                    else:
                        pu = u
                        ppar = ppar_
                        pex = s > 0
                    info.append((u, pu, ppar, pex))
                return info

            def emit_S2(step):
                (pz, s, g, c, H, br, item) = step["key"]
                par = g % 3
                qs = g % 2
                dl = DILS[br]
                units = (2 * item, 2 * item + 1) if dl != 16 else tuple(4 * item + i for i in range(4))
                if dl != 16:
                    units = tuple(8 * H + u for u in units)
                info = job_info(pz, s, g, dl, units)
                step["info"] = info
                step["pi"] = []
                mms = [[], []]
                for e in range(2):
                    hi = 2 * c + e
                    pb = e * 64
                    qci = c
                    kci = 0 if pz["grp"] == "A" else c
                    for ui, (u, pu, ppar, pex) in enumerate(info):
                        off = unit_off(dl, u)
                        kcols_d = slice(off, off + dl * 127 + 1, dl)
                        if dl == 1:
                            qsrc, qbuf, qcols, w = qt[qs], qt_b[qs], slice(off, off + 128), 128
                        elif dl == 4:
                            qsrc, qbuf, qcols, w = qd[0][qs], qd_b[0][qs], slice(u * 128, u * 128 + 128), 128
                        else:
                            qsrc, qbuf, w = qd[1][qs], qd_b[1][qs], 64
                            qcols = slice(u * 128 + 64 * H, u * 128 + 64 * H + 64)
                        if pex:
                            poff = unit_off(dl, pu)
                            kp, kcols_p = ppar, slice(poff, poff + dl * 127 + 1, dl)
                        else:
                            kp, kcols_p = par, kcols_d
                        base = ui * 2 * w
                        mms[e].append((kt_b[kp], qbuf, slice(base, base + w), kt[kp][pb:pb + 64, kci, kcols_p],
                                       qsrc[pb:pb + 64, qci, qcols]))
                        mms[e].append((kt_b[par], qbuf, slice(base + w, base + 2 * w),
                                       kt[par][pb:pb + 64, kci, kcols_d], qsrc[pb:pb + 64, qci, qcols]))
                n = len(mms[0])
                for i in range(n):
                    for e in range(2):
                        (kb, qb_, ocols, lhsT, rhs) = mms[e][i]
                        PE.run([kb, qb_], [sps_b[e]],
                               lambda h: h.matmul(sps[e][:, ocols], lhsT=lhsT, rhs=rhs, start=True, stop=True),
                               mark=(i == n - 1))
                for e in range(2):
                    hi = 2 * c + e
                    mi = 0 if pz["grp"] == "A" else br * 4 + hi
                    ss_i = nxt("ss", NSS)
                    pi_ = nxt("p", NPT)
                    step["pi"].append(pi_)
                    if dl != 16:
                        DVE.run([sps_b[e], masks_b], [ssb_b[ss_i]],
                                lambda h: h.tensor_tensor(
                                    out=ssb[ss_i][:].rearrange("p (u c) -> p u c", u=2),
                                    in0=sps[e][:, :].rearrange("p (u c) -> p u c", u=2),
                                    in1=masks[:, mi:mi + 1, :].broadcast_to([128, 2, 256]), op=ALU.add))
                    else:
                        msl = masks[:, mi:mi + 1, :].rearrange("p o (a q) -> p o a q", a=2)[:, :, :,
                                                                                           64 * H:64 * H + 64]
                        DVE.run([sps_b[e], masks_b], [ssb_b[ss_i]],
                                lambda h: h.tensor_tensor(
                                    out=ssb[ss_i][:].rearrange("p (u a q) -> p u a q", u=4, a=2),
                                    in0=sps[e][:, :].rearrange("p (u a q) -> p u a q", u=4, a=2),
                                    in1=msl.broadcast_to([128, 4, 2, 64]), op=ALU.add))
                    ACT.run([ssb_b[ss_i]], [pT_b[pi_]],
                            lambda h: h.activation(out=pT[pi_][:], in_=ssb[ss_i][:], func=AF.Exp, scale=pz["scale"]))

            def emit_PV2(step, first):
                (pz, s, g, c, H, br, item) = step["key"]
                par = g % 3
                dl = DILS[br]
                for e in range(2):
                    hi = 2 * c + e
                    vh = e if pz["grp"] == "A" else hi
                    pi_ = step["pi"][e]
                    pvs = []
                    for ui, (u, pu, ppar, pex) in enumerate(step["info"]):
                        for part in (0, 1):
                            if part == 0 and not pex:
                                continue
                            vslot = ppar if part == 0 else par
                            vu = pu if part == 0 else u
                            if dl == 1:
                                lo = (u - 8 * H) * 128
                                pvs.append((lo // 512, slice(lo % 512, lo % 512 + 128), vslot, vu,
                                            ui * 256 + part * 128, 128))
                            elif dl == 4:
                                r = u % 4
                                pvs.append((u // 4 - 2 * H, slice(r, r + 4 * 127 + 1, 4), vslot, vu,
                                            ui * 256 + part * 128, 128))
                            else:
                                for q2 in range(2):
                                    pvs.append((q2, slice(u, u + 16 * 31 + 1, 16), vslot, vu,
                                                ui * 128 + part * 64 + 32 * q2, 32))
                    for ii, (bank, cols, vslot, vu, p0, n) in enumerate(pvs):
                        st_ = first[e][bank]
                        first[e][bank] = False
                        PE.run([vt_b[br][vslot], pT_b[pi_]], [acc_b[2 * e + bank]],
                               lambda h: h.matmul(acc[2 * e + bank][0:65, cols],
                                                  lhsT=vt[br][vslot][:, vu, vh * 65:(vh + 1) * 65],
                                                  rhs=pT[pi_][:, p0:p0 + n], start=st_, stop=False,
                                                  skip_group_check=True),
                               mark=(ii == len(pvs) - 1))

            def emit_evac2():
                for e in range(2):
                    for bank in range(2):
                        dst = accsb[0][0:65, e * 1024 + bank * 512:e * 1024 + (bank + 1) * 512]
                        if e == 0:
                            ACT.run([acc_b[2 * e + bank]], [accsb_b[0]],
                                    lambda h: h.activation(out=dst, in_=acc[2 * e + bank][0:65, :], func=AF.Copy))
                        else:
                            DVE.run([acc_b[2 * e + bank]], [accsb_b[0]],
                                    lambda h: h.tensor_copy(out=dst, in_=acc[2 * e + bank][0:65, :]))

            def emit_tr2(c, H):
                for e in range(2):
                    hi = 2 * c + e
                    for gq in range(2):
                        ti = nxt("tp", 2)
                        for i in range(4):
                            tl = 4 * gq + i
                            PE.run([accsb_b[0], cst_b], [tpu_b[ti]],
                                   lambda h: h.transpose(
                                       out=tpu[ti][:, i, :],
                                       in_=accsb[0][0:65, e * 1024 + tl * 128:e * 1024 + (tl + 1) * 128],
                                       identity=idf_t[0:65, 0:65]), mark=(i == 3))
                        ACT.run([tpu_b[ti]], [usb_b[0]],
                                lambda h: h.activation(
                                    out=usb[0][:, 8 * H + 4 * gq:8 * H + 4 * gq + 4, hi, :], in_=tpu[ti],
                                    func=AF.Copy))

            spans = [(pz, s) for pz in passes for s in range(NSP)]
            load_masks(spans[0][0])
            load_span(spans[0][0], spans[0][1], 0)
            destride(spans[0][0], 0)
            for g, (pz, s) in enumerate(spans):
                if True:
                    base = s * 2048
                    if s == 0 and g > 0:
                        load_masks(pz)
                    nxt_span = spans[g + 1] if g + 1 < len(spans) else None
                    if nxt_span is not None:
                        load_span(nxt_span[0], nxt_span[1], g + 1)
                    groups = [(c, H) for c in range(2) for H in range(2)]
                    steps = []
                    for (c, H) in groups:
                        for br in pz["brs"]:
                            for item in range(4):
                                steps.append(dict(key=(pz, s, g, c, H, br, item), grp=(c, H)))
                    first = [[True, True], [True, True]]
                    pend_tr = None
                    LA = 3
                    nj = len(steps)
                    for idx in range(nj + LA):
                        if idx == nj // 4 and nxt_span is not None:
                            destride(nxt_span[0], g + 1)
                        if idx < nj:
                            emit_S2(steps[idx])
                        if idx >= LA:
                            pj = steps[idx - LA]
                            emit_PV2(pj, first)
                            if pend_tr is not None:
                                emit_tr2(*pend_tr)
                                pend_tr = None
                            if idx - LA + 1 >= nj or steps[idx - LA + 1]["grp"] != pj["grp"]:
                                emit_evac2()
                                pend_tr = pj["grp"]
                                first = [[True, True], [True, True]]
                    if pend_tr is not None:
                        emit_tr2(*pend_tr)
                    c0 = pz["uslot"] * 260
                    for hh in range(2):
                        cx.dma(usc[base + hh * 1024:base + (hh + 1) * 1024, c0:c0 + 260].rearrange(
                                   "(u j) c -> j u c", j=128),
                               usb[0][:, 8 * hh:8 * hh + 8, :, :].rearrange("p u h d -> p u (h d)"),
                               [usb_b[0]], [], usb_s[0])
            cx.barrier()

    def phase_c():
        with contextlib.ExitStack() as ph:
            def sb(name, shape, dt):
                return ph.enter_context(nc.sbuf_tensor(name, list(shape), dt))

            def ps(name, shape, dt=F32):
                return ph.enter_context(nc.psum_tensor(name, list(shape), dt))

            Wo = sb("Wo", [128, 8, D], BF16)
            Wg = sb("Wg", [128, 8, DFF], BF16)
            Wu = sb("Wu", [128, 8, DFF], BF16)
            Wd = sb("Wd", [128, 22, D], BF16)
            Wo_b, Wg_b, Wu_b, Wd_b = Buf("Wo"), Buf("Wg"), Buf("Wu"), Buf("Wd")
            idf_t = sb("c_idf", [128, 128], F32)
            idb_t = sb("c_idb", [128, 128], BF16)
            gfin_t = sb("gfin_t", [128, D], F32)
            ex16 = sb("ex16", [128, 16], F32)
            cst_b = Buf("cconst")
            dsc = cx.dsem()
            cx.dma(idf_t[:], identf[:, :], [], [cst_b], dsc)
            cx.dma(gfin_t[:], gfin_bc[:, :], [], [cst_b], dsc)
            cx.dma(ex16[:, 0:8], sinks_bc[:, :], [], [cst_b], dsc)
            DVE.run([cst_b], [cst_b], lambda h: h.tensor_copy(out=idb_t[:], in_=idf_t[:]))
            ACT.run([cst_b], [cst_b], lambda h: h.activation(out=ex16[:, 0:8], in_=ex16[:, 0:8], func=AF.Exp))
            DVE.run([cst_b], [cst_b], lambda h: h.memset(ex16[:, 8:16], 0.0))
            uin = [sb("uin%d" % i, [128, 16, 65], F32) for i in range(2)]
            uin_b = [Buf() for _ in range(2)]
            uin_s = [cx.dsem() for _ in range(2)]
            NX1 = 4
            x1 = [sb("x1_%d" % i, [128, D], F32) for i in range(NX1)]
            x1_b = [Buf() for _ in range(NX1)]
            x1_s = [cx.dsem() for _ in range(NX1)]

            def load_u(R):
                for tt in range(2):
                    t = 2 * R + tt
                    ui = t % 2
                    cx.dma(uin[ui][:].rearrange("p h d -> p (h d)"), usc[t * 128:(t + 1) * 128, :], [],
                           [uin_b[ui]], uin_s[ui])

            def load_x(R):
                for tt in range(2):
                    t = 2 * R + tt
                    xi = t % NX1
                    cx.dma(x1[xi][:], x[t * 128:(t + 1) * 128, :], [], [x1_b[xi]], x1_s[xi])

            ws = [cx.dsem() for _ in range(4)]
            load_u(0)
            cx.dma(Wo[:], wbf_o.rearrange("(k p) c -> p k c", p=128), [], [Wo_b], ws[0])
            load_x(0)
            for k0 in range(0, 8, 2):
                cx.dma(Wg[:, k0:k0 + 2, :], wbf_g[k0 * 128:(k0 + 2) * 128, :].rearrange("(k p) c -> p k c", p=128),
                       [], [Wg_b], ws[1])
                cx.dma(Wu[:, k0:k0 + 2, :], wbf_u[k0 * 128:(k0 + 2) * 128, :].rearrange("(k p) c -> p k c", p=128),
                       [], [Wu_b], ws[2])
            for f0 in range(0, 22, 11):
                cx.dma(Wd[:, f0:f0 + 11, :],
                       wbf_d[f0 * 128:(f0 + 11) * 128, :].rearrange("(k p) c -> p k c", p=128), [], [Wd_b], ws[3])
            junk = sb("cjunk", [128, D], mybir.dt.float8e4)
            st = [sb("cst%d" % i, [128, 4], F32) for i in range(8)]
            st_b = [Buf() for _ in range(8)]
            den = sb("den", [128, 32], F32)
            den_b = Buf()
            ot = sb("ot", [128, 16, 64], F32)
            ot_b = Buf()
            mx = sb("mx", [128, D], BF16)
            mx_b = Buf()
            tp = [ps("ctp%d" % i, [128, D], BF16) for i in range(2)]
            tp_b = [Buf() for _ in range(2)]
            mT = sb("mT", [128, 8, 128], BF16)
            mT_b = Buf()
            h2T = [sb("h2T%d" % i, [128, 8, 256], BF16) for i in range(2)]
            h2T_b = [Buf() for _ in range(2)]
            sg = [sb("sg%d" % i, [128, 256], F32) for i in range(2)]
            sg_b = [Buf() for _ in range(2)]
            actT = sb("actT", [128, 22, 256], BF16)
            actT_b = Buf()
            NPM, NPG = 2, 4
            pm = [ps("pm%d" % i, [128, 512], F32) for i in range(NPM)]
            pm_b = [Buf() for _ in range(NPM)]
            pg = [ps("pg%d" % i, [128, 2, 256], F32) for i in range(NPG)]
            pg_b = [Buf() for _ in range(NPG)]
            cnt = dict(pm=0, pg=0, tp=0, st=0, sg=0)

            def nxt(key, n):
                i = cnt[key] % n
                cnt[key] += 1
                return i

            def rms_stats(src_ap, src_b, si, ncols):
                ACT.run([src_b], [st_b[si]],
                        lambda h: h.activation(out=junk[:, 0:ncols], in_=src_ap, func=AF.Square,
                                               accum_out=st[si][:, 0:1], saturate=False))
                ACT.run([], [st_b[si]],
                        lambda h: h.activation(out=st[si][:, 1:2], in_=st[si][:, 0:1], func=AF.Sqrt,
                                               scale=1.0 / ncols, bias=EPS))
                DVE.run([], [st_b[si]], lambda h: h.reciprocal(out=st[si][:, 2:3], in_=st[si][:, 1:2]))

            def front(R):
                hs = R % 2
                for tt in range(2):
                    t = 2 * R + tt
                    ui = t % 2
                    xi = t % NX1
                    DVE.run([uin_b[ui], cst_b], [den_b],
                            lambda h: h.tensor_tensor(out=den[:, 0:16], in0=uin[ui][:, :, 64], in1=ex16[:],
                                                      op=ALU.add))
                    DVE.run([], [den_b], lambda h: h.reciprocal(out=den[:, 16:32], in_=den[:, 0:16]))
                    DVE.run([uin_b[ui], den_b], [ot_b],
                            lambda h: h.tensor_tensor(
                                out=ot[:], in0=uin[ui][:, :, 0:64],
                                in1=den[:, 16:32].unsqueeze(2).broadcast_to([128, 16, 64]), op=ALU.mult))
                    yield
                    sA = nxt("st", 8)
                    rms_stats(ot[:, 0:8, :].rearrange("p h d -> p (h d)"), ot_b, sA, 512)
                    sB = nxt("st", 8)
                    rms_stats(ot[:, 8:16, :].rearrange("p h d -> p (h d)"), ot_b, sB, 512)
                    ACT.run([ot_b, st_b[sA]], [mx_b],
                            lambda h: h.activation(out=mx[:, 0:512], in_=ot[:, 0:8, :].rearrange("p h d -> p (h d)"),
                                                   func=AF.Copy, scale=st[sA][:, 2:3]))
                    DVE.run([ot_b, st_b[sB]], [mx_b],
                            lambda h: h.tensor_scalar(out=mx[:, 512:1024],
                                                      in0=ot[:, 8:16, :].rearrange("p h d -> p (h d)"),
                                                      scalar1=st[sB][:, 2:3], scalar2=None, op0=ALU.mult))
                    yield
                    yield
                    ti = nxt("tp", 2)
                    for k in range(8):
                        PE.run([mx_b, cst_b], [tp_b[ti]],
                               lambda h: h.transpose(out=tp[ti][:, k * 128:(k + 1) * 128],
                                                     in_=mx[:, k * 128:(k + 1) * 128], identity=idb_t[:]),
                               mark=(k == 7))
                    ACT.run([tp_b[ti]], [mT_b],
                            lambda h: h.activation(out=mT[:], in_=tp[ti][:].rearrange("p (k t) -> p k t", k=8),
                                                   func=AF.Copy))
                    yield
                    yield
                    for half in range(2):
                        pi_ = nxt("pm", NPM)
                        for k in range(8):
                            PE.run([mT_b, Wo_b], [pm_b[pi_]],
                                   lambda h: h.matmul(pm[pi_][:, :], lhsT=mT[:, k, :],
                                                      rhs=Wo[:, k, half * 512:(half + 1) * 512],
                                                      start=(k == 0), stop=(k == 7)), mark=(k == 7))
                        DVE.run([pm_b[pi_]], [x1_b[xi]],
                                lambda h: h.tensor_tensor(out=x1[xi][:, half * 512:(half + 1) * 512],
                                                          in0=pm[pi_][:, :],
                                                          in1=x1[xi][:, half * 512:(half + 1) * 512], op=ALU.add))
                    s2 = nxt("st", 8)
                    rms_stats(x1[xi][:], x1_b[xi], s2, D)
                    DVE.run([x1_b[xi], st_b[s2]], [mx_b],
                            lambda h: h.tensor_scalar(out=mx[:], in0=x1[xi][:], scalar1=st[s2][:, 2:3],
                                                      scalar2=None, op0=ALU.mult))
                    yield
                    yield
                    yield
                    ti = nxt("tp", 2)
                    for k in range(8):
                        PE.run([mx_b, cst_b], [tp_b[ti]],
                               lambda h: h.transpose(out=tp[ti][:, k * 128:(k + 1) * 128],
                                                     in_=mx[:, k * 128:(k + 1) * 128], identity=idb_t[:]),
                               mark=(k == 7))
                    ACT.run([tp_b[ti]], [h2T_b[hs]],
                            lambda h: h.activation(out=h2T[hs][:, :, tt * 128:(tt + 1) * 128],
                                                   in_=tp[ti][:].rearrange("p (k t) -> p k t", k=8), func=AF.Copy))
                    yield

            def ffn(R, gen):
                hs = R % 2
                for f in range(22):
                    gi = nxt("pg", NPG)
                    for w_i, (W, Wb) in enumerate(((Wg, Wg_b), (Wu, Wu_b))):
                        for k in range(8):
                            PE.run([h2T_b[hs], Wb], [pg_b[gi]],
                                   lambda h: h.matmul(pg[gi][:, w_i, :], lhsT=W[:, k, f * 128:(f + 1) * 128],
                                                      rhs=h2T[hs][:, k, :], start=(k == 0), stop=(k == 7)),
                                   mark=(k == 7 and w_i == 1))
                    gs = nxt("sg", 2)
                    ACT.run([pg_b[gi]], [sg_b[gs]],
                            lambda h: h.activation(out=sg[gs][:], in_=pg[gi][:, 0, :], func=AF.Silu))
                    DVE.run([pg_b[gi], sg_b[gs]], [actT_b],
                            lambda h: h.tensor_tensor(out=actT[:, f, :], in0=pg[gi][:, 1, :], in1=sg[gs][:],
                                                      op=ALU.mult))
                    if gen is not None and f >= 3:
                        next(gen, None)
                if gen is not None:
                    for _ in gen:
                        pass
                for tt in range(2):
                    t = 2 * R + tt
                    xi = t % NX1
                    for half in range(2):
                        pi_ = nxt("pm", NPM)
                        for f in range(22):
                            PE.run([actT_b, Wd_b], [pm_b[pi_]],
                                   lambda h: h.matmul(pm[pi_][:, :], lhsT=actT[:, f, tt * 128:(tt + 1) * 128],
                                                      rhs=Wd[:, f, half * 512:(half + 1) * 512],
                                                      start=(f == 0), stop=(f == 21)), mark=(f == 21))
                        DVE.run([pm_b[pi_]], [x1_b[xi]],
                                lambda h: h.tensor_tensor(out=x1[xi][:, half * 512:(half + 1) * 512],
                                                          in0=pm[pi_][:, :],
                                                          in1=x1[xi][:, half * 512:(half + 1) * 512], op=ALU.add))
                    s3 = nxt("st", 8)
                    rms_stats(x1[xi][:], x1_b[xi], s3, D)
                    DVE.run([st_b[s3], cst_b], [x1_b[xi]],
                            lambda h: h.scalar_tensor_tensor(out=x1[xi][:], in0=x1[xi][:], scalar=st[s3][:, 2:3],
                                                             in1=gfin_t[:], op0=ALU.mult, op1=ALU.mult))
                    cx.dma(y[t * 128:(t + 1) * 128, :], x1[xi][:], [x1_b[xi]], [], x1_s[xi])

            if NR > 1:
                load_x(1)
            for _ in front(0):
                pass
            if NR > 1:
                load_u(1)
            for R in range(NR):
                gen = front(R + 1) if R + 1 < NR else None
                ffn(R, gen)
                if R + 2 < NR:
                    load_u(R + 2)
                    load_x(R + 2)
            cx.barrier()

    if "A" in phases:
        phase_a()
    if "B" in phases:
        phase_b()
    if "C" in phases:
        phase_c()
    return nc


def _colperm():
    cols = []
    for c in range(4):
        for hd in (c, 4 + c):
            cols += list(range(hd * 64, hd * 64 + 64))
    cols += list(range(512, 640))
    cols += list(range(768, 1280))
    cols += list(range(1280, 1792))
    cols += list(range(640, 768))
    cols += list(range(1792, 2304))
    for c in range(4):
        for hd in (c, 4 + c):
            b = hd * 64
            cols += list(range(b + 32, b + 64)) + list(range(b, b + 32))
    for kv in range(2):
        b = 512 + kv * 64
        cols += list(range(b + 32, b + 64)) + list(range(b, b + 32))
    return np.array(cols, dtype=np.int64)


def _t5_bucket(dist):
    dist = np.asarray(dist, dtype=np.int64)
    max_exact = 16
    df = np.maximum(dist, 1).astype(np.float32)
    large = max_exact + (np.log(df / np.float32(max_exact)) / np.float32(np.log(2048 / 16))
                         * np.float32(32 - max_exact)).astype(np.int32)
    large = np.minimum(large, 31)
    return np.where(dist < max_exact, dist, large)


def _host_prep(inputs, S):
    f32 = np.float32
    cp = _colperm()
    w_in = np.asarray(inputs["w_in"], dtype=f32)[0]
    b_in = np.asarray(inputs["b_in"], dtype=f32)[0]
    w_in_p = np.ascontiguousarray(w_in[:, cp])
    b_p = b_in[cp]
    chunks = list(range(13)) + [18 + i for i in range(5)]
    b_fm = np.stack([b_p[c * 128:(c + 1) * 128] for c in chunks], axis=1).astype(f32)
    b_v = np.ascontiguousarray(np.broadcast_to(b_p[1664:2304][None, :], (128, 640))).astype(f32)

    def pm(v):
        return np.ascontiguousarray(np.asarray(v, dtype=f32).reshape(8, 128).T)

    gat = pm(inputs["g_attn"][0])
    gffn = pm(inputs["g_ffn"][0])
    rowperm = []
    for hd in A_ORDER:
        rowperm += list(range(hd * 64, hd * 64 + 64))
    rowperm += list(range(512, 1024))
    rowperm = np.array(rowperm)
    g_out = np.concatenate([np.asarray(inputs["g_out_a"], f32)[0], np.asarray(inputs["g_out_b"], f32)[0]])
    gout = pm(g_out[rowperm])
    w_o_p = np.ascontiguousarray(np.asarray(inputs["w_o"], f32)[0][rowperm, :])
    sinks = np.asarray(inputs["sinks"], f32)[0][A_ORDER]
    sinks_bc = np.ascontiguousarray(np.broadcast_to(sinks[None, :], (128, 8))).astype(f32)
    gfin_bc = np.ascontiguousarray(np.broadcast_to(np.asarray(inputs["g_final"], f32)[None, :], (128, D)))
    half = 32
    inv_freq = (150000.0 ** (-np.arange(half, dtype=np.float64) / half)).astype(f32)
    ang = (np.arange(S, dtype=f32)[:, None] * inv_freq[None, :]).astype(f32)
    cosv = np.cos(ang).astype(f32).T
    sinv = np.sin(ang).astype(f32).T
    cs = np.empty((2, 128, S), f32)
    for p in range(128):
        i = p % 32
        hf = (p % 64) // 32
        cs[0, p] = cosv[i]
        cs[1, p] = -sinv[i] if hf == 0 else sinv[i]
    rel = np.asarray(inputs["rel_table"], f32)
    sidx = np.arange(128)[:, None]
    qidx = np.arange(128)[None, :]
    d_prev = 128 + qidx - sidx
    d_diag = qidx - sidx
    maskb = np.empty((128, 25, 256), f32)
    maskb[:, 0, 0:128] = np.where(d_prev <= 127, 0.0, NEG)
    maskb[:, 0, 128:256] = np.where(d_diag >= 0, 0.0, NEG)
    for br, dl in enumerate(DILS):
        bp = _t5_bucket(np.clip(d_prev, 0, 128) * dl)
        bd = _t5_bucket(np.clip(d_diag, 0, 128) * dl)
        for hd in range(8):
            mi = 1 + br * 8 + hd
            maskb[:, mi, 0:128] = np.where(d_prev <= 128, rel[bp, hd], NEG)
            maskb[:, mi, 128:256] = np.where(d_diag >= 0, rel[bd, hd], NEG)
    shared = dict(
        w_in_p=w_in_p, b_fm=b_fm, b_v=b_v, gat=gat, cs=cs, maskb=maskb, identf=np.eye(128, dtype=f32),
        sinks_bc=sinks_bc, w_o_p=w_o_p, gout=gout, gffn=gffn,
        w_gate=np.ascontiguousarray(np.asarray(inputs["w_gate"], f32)[0]),
        w_up=np.ascontiguousarray(np.asarray(inputs["w_up"], f32)[0]),
        w_down=np.ascontiguousarray(np.asarray(inputs["w_down"], f32)[0]),
        gfin_bc=gfin_bc,
    )
    return shared


_NC_CACHE = {}


def kernel(x, g_attn, w_in, b_in, sinks, rel_table, g_out_a, g_out_b, w_o, g_ffn, w_gate, w_up, w_down,
           g_final):
    inputs = dict(x=x, g_attn=g_attn, w_in=w_in, b_in=b_in, sinks=sinks, rel_table=rel_table, g_out_a=g_out_a,
                  g_out_b=g_out_b, w_o=w_o, g_ffn=g_ffn, w_gate=w_gate, w_up=w_up, w_down=w_down,
                  g_final=g_final)
    x = np.asarray(x, dtype=np.float32)
    B, S, _ = x.shape
    shared = _host_prep(inputs, S)
    if S not in _NC_CACHE:
        _NC_CACHE[S] = build(S)
    nc = _NC_CACHE[S]
    in_maps = []
    for b in range(B):
        m = dict(shared)
        m["x"] = np.ascontiguousarray(x[b])
        in_maps.append(m)
    res = run_bass_kernel_spmd(nc, in_maps, core_ids=list(range(B)))
    out = np.stack([np.asarray(r["y"], dtype=np.float32) for r in res.results], axis=0)
    return out
```

```python
import contextlib
import numpy as np
import concourse.bass as bass
import concourse.mybir as mybir
from concourse.bass_utils import run_bass_kernel_spmd

F32 = mybir.dt.float32
BF16 = mybir.dt.bfloat16
AF = mybir.ActivationFunctionType
ALU = mybir.AluOpType

D = 1024
HD = 64
DFF = 2816
EPS = 1e-5
NEG = -30000.0
NCOLW = 2944
A_ORDER = [0, 4, 1, 5, 2, 6, 3, 7]
DILS = (1, 4, 16)


class Tk:
    __slots__ = ("sem", "val")

    def __init__(self, sem, val):
        self.sem = sem
        self.val = val


class Buf:
    __slots__ = ("name", "w", "r")

    def __init__(self, name=""):
        self.name = name
        self.w = None
        self.r = {}


class Eng:
    def __init__(self, name, h, sem, is_pe=False):
        self.name = name
        self.h = h
        self.sem = sem
        self.n = 0
        self.seen = {}
        self.is_pe = is_pe
        self.pending = []

    def wait(self, tk):
        if tk is None:
            return
        if self.is_pe and tk.sem is self.sem:
            return
        k = id(tk.sem)
        if self.seen.get(k, 0) >= tk.val:
            return
        self.h.wait_ge(tk.sem, tk.val)
        self.seen[k] = tk.val

    def deps(self, reads, writes):
        for b in reads:
            self.wait(b.w)
        for b in writes:
            self.wait(b.w)
            for t in list(b.r.values()):
                self.wait(t)

    @staticmethod
    def commit(tk, reads, writes):
        k = id(tk.sem)
        for b in reads:
            b.r[k] = tk
        for b in writes:
            b.w = tk
            b.r = {}

    def run(self, reads, writes, fn, mark=True):
        self.deps(reads, writes)
        ins = fn(self.h)
        if self.is_pe and not mark:
            self.pending.append((reads, writes))
            return None
        self.n += 1
        ins.then_inc(self.sem, 1)
        tk = Tk(self.sem, self.n)
        for (r, w) in self.pending:
            self.commit(tk, r, w)
        self.pending = []
        self.commit(tk, reads, writes)
        return tk


class DSem:
    def __init__(self, sem):
        self.sem = sem
        self.n = 0


class Ctx:
    def __init__(self, nc):
        self.nc = nc
        self.pe = Eng("pe", nc.tensor, nc.alloc_semaphore("c_pe"), is_pe=True)
        self.act = Eng("act", nc.scalar, nc.alloc_semaphore("c_act"))
        self.dve = Eng("dve", nc.vector, nc.alloc_semaphore("c_dve"))
        self.pool = Eng("pool", nc.gpsimd, nc.alloc_semaphore("c_pool"))
        self.sp = Eng("sp", nc.sync, None)
        self.nsem = 0
        self.dsems = []

    def dsem(self):
        self.nsem += 1
        d = DSem(self.nc.alloc_semaphore("d%d" % self.nsem))
        self.dsems.append(d)
        return d

    def dma(self, out, in_, reads, writes, ds):
        q = self.sp
        q.deps(reads, writes)
        ins = q.h.dma_start(out=out, in_=in_)
        ds.n += 16
        ins.then_inc(ds.sem, 16)
        tk = Tk(ds.sem, ds.n)
        Eng.commit(tk, reads, writes)
        return tk

    def barrier(self):
        tks = []
        for e in (self.pe, self.act, self.dve, self.pool):
            assert not e.pending
            if e.n:
                tks.append(Tk(e.sem, e.n))
        for d in self.dsems:
            if d.n:
                tks.append(Tk(d.sem, d.n))
        for e in (self.pe, self.act, self.dve, self.pool, self.sp):
            for t in tks:
                if e.is_pe and t.sem is e.sem:
                    continue
                k = id(t.sem)
                if e.seen.get(k, 0) >= t.val:
                    continue
                e.h.wait_ge(t.sem, t.val)
                e.seen[k] = t.val


def build(S=8192, dbg=False, phases="ABC", stop=None):
    NT = S // 512
    NTT = S // 128
    NSP = S // 2048
    NR = S // 256
    nc = bass.Bass("TRN2", target_bir_lowering=False)
    cx = Ctx(nc)
    PE, ACT, DVE, POOL = cx.pe, cx.act, cx.dve, cx.pool

    def din(name, shape, dt=F32):
        return nc.dram_tensor(name, list(shape), dt, kind="ExternalInput").ap()

    x = din("x", [S, D])
    w_in_p = din("w_in_p", [D, NCOLW])
    b_fm = din("b_fm", [128, 18])
    b_v = din("b_v", [128, 640])
    gat = din("gat", [128, 8])
    cs = din("cs", [2, 128, S])
    maskb = din("maskb", [128, 25, 256])
    identf = din("identf", [128, 128])
    sinks_bc = din("sinks_bc", [128, 8])
    w_o_p = din("w_o_p", [D, D])
    gout = din("gout", [128, 8])
    gffn = din("gffn", [128, 8])
    w_gate = din("w_gate", [D, DFF])
    w_up = din("w_up", [D, DFF])
    w_down = din("w_down", [DFF, D])
    gfin_bc = din("gfin_bc", [128, D])
    okind = "ExternalOutput"
    y = nc.dram_tensor("y", [S, D], F32, kind=okind).ap()
    skind = "ExternalOutput" if dbg else "Internal"
    qkt = nc.dram_tensor("qkt", [13, 128, S], BF16, kind=skind).ap()
    vsc = nc.dram_tensor("vsc", [S, 650], BF16, kind=skind).ap()
    usc = nc.dram_tensor("usc", [S, 16 * 65], F32, kind=skind).ap()
    wbf_o = nc.dram_tensor("wbf_o", [D, D], BF16).ap()
    wbf_g = nc.dram_tensor("wbf_g", [D, DFF], BF16).ap()
    wbf_u = nc.dram_tensor("wbf_u", [D, DFF], BF16).ap()
    wbf_d = nc.dram_tensor("wbf_d", [DFF, D], BF16).ap()

    def phase_a():
        with contextlib.ExitStack() as ph:
            def sb(name, shape, dt):
                return ph.enter_context(nc.sbuf_tensor(name, list(shape), dt))

            def ps(name, shape, dt=F32):
                return ph.enter_context(nc.psum_tensor(name, list(shape), dt))

            Wp = sb("Wp", [128, 8, NCOLW], BF16)
            Wp_b = Buf("Wp")
            gat_t = sb("gat_t", [128, 8], F32)
            bfm_t = sb("bfm_t", [128, 18], F32)
            bv_t = sb("bv_t", [128, 640], F32)
            idf_t = sb("idf_t", [128, 128], F32)
            idb_t = sb("idb_t", [128, 128], BF16)
            cst_b = Buf("consts")
            dsc = cx.dsem()
            cx.dma(gat_t[:], gat[:, :], [], [cst_b], dsc)
            cx.dma(bfm_t[:], b_fm[:, :], [], [cst_b], dsc)
            cx.dma(bv_t[:], b_v[:, :], [], [cst_b], dsc)
            cx.dma(idf_t[:], identf[:, :], [], [cst_b], dsc)
            gout_a = sb("gout_a", [128, 8], F32)
            gffn_a = sb("gffn_a", [128, 8], F32)
            cx.dma(gout_a[:], gout[:, :], [], [cst_b], dsc)
            cx.dma(gffn_a[:], gffn[:, :], [], [cst_b], dsc)
            idb_b = Buf("idb")
            DVE.run([cst_b], [idb_b], lambda h: h.tensor_copy(out=idb_t[:], in_=idf_t[:]))
            with contextlib.ExitStack() as prep:
                stg = [prep.enter_context(nc.sbuf_tensor("wstg%d" % i, [128, NCOLW], F32)) for i in range(2)]
                stg_b = [Buf("wstg%d" % i) for i in range(2)]
                stg_s = [cx.dsem() for _ in range(2)]
                for k in range(8):
                    i = k % 2
                    cx.dma(stg[i][:], w_in_p[k * 128:(k + 1) * 128, :], [], [stg_b[i]], stg_s[i])
                    hc = NCOLW // 2
                    DVE.run([stg_b[i], cst_b], [Wp_b],
                            lambda h: h.tensor_scalar(out=Wp[:, k, 0:hc], in0=stg[i][:, 0:hc],
                                                      scalar1=gat_t[:, k:k + 1], scalar2=None, op0=ALU.mult))
                    ACT.run([stg_b[i], cst_b], [Wp_b],
                            lambda h: h.activation(out=Wp[:, k, hc:NCOLW], in_=stg[i][:, hc:NCOLW], func=AF.Copy,
                                                   scale=gat_t[:, k:k + 1]))
                cx.barrier()
            if stop == "prep":
                return
            NXS = 4
            xt = [sb("xt%d" % i, [128, D], F32) for i in range(NXS)]
            xt_b = [Buf() for _ in range(NXS)]
            xt_s = [cx.dsem() for _ in range(NXS)]
            junk = sb("junk", [128, D], BF16)
            st = [sb("st%d" % i, [128, 4], F32) for i in range(4)]
            st_b = [Buf() for _ in range(4)]
            hb = [sb("hb%d" % i, [128, D], BF16) for i in range(2)]
            hb_b = [Buf() for _ in range(2)]
            tp = [ps("tp%d" % i, [128, D], BF16) for i in range(2)]
            tp_b = [Buf() for _ in range(2)]
            hT = [sb("hT%d" % i, [128, 8, 512], BF16) for i in range(2)]
            hT_b = [Buf() for _ in range(2)]
            cst = [sb("cs%d" % i, [128, 2, 512], F32) for i in range(2)]
            cs_b = [Buf() for _ in range(2)]
            cs_s = [cx.dsem() for _ in range(2)]
            NMM = 6
            mm = [ps("mm%d" % i, [128, 512], F32) for i in range(NMM)]
            mm_b = [Buf() for _ in range(NMM)]
            t1 = [sb("t1_%d" % i, [128, 512], F32) for i in range(2)]
            t1_b = [Buf() for _ in range(2)]
            t2 = [sb("t2_%d" % i, [128, 512], F32) for i in range(2)]
            t2_b = [Buf() for _ in range(2)]
            NO = 4
            ob = [sb("ob%d" % i, [128, 512], BF16) for i in range(NO)]
            ob_b = [Buf() for _ in range(NO)]
            ob_s = [cx.dsem() for _ in range(NO)]
            vp = [sb("vp%d" % i, [128, 10, 65], BF16) for i in range(2)]
            vp_b = [Buf() for _ in range(2)]
            vp_s = [cx.dsem() for _ in range(2)]
            for i in range(2):
                POOL.run([], [vp_b[i]], lambda h: h.memset(vp[i][:], 1.0))

            def load_x(t):
                i = t % NXS
                cx.dma(xt[i][:], x[t * 128:(t + 1) * 128, :], [], [xt_b[i]], xt_s[i])

            for t in range(min(NXS, NTT)):
                load_x(t)
            cnt = dict(mm=0, o=0, rope=0)

            def nxt(key, n):
                i = cnt[key] % n
                cnt[key] += 1
                return i

            def front(T):
                Ts = T % 2
                cx.dma(cst[Ts][:, 0, :], cs[0, :, T * 512:(T + 1) * 512], [], [cs_b[Ts]], cs_s[Ts])
                cx.dma(cst[Ts][:, 1, :], cs[1, :, T * 512:(T + 1) * 512], [], [cs_b[Ts]], cs_s[Ts])

                def stage_e(tt):
                    t = 4 * T + tt
                    xi = t % NXS
                    si = t % 4
                    hi = t % 2
                    ACT.run([xt_b[xi]], [st_b[si]],
                            lambda h: h.activation(out=junk[:], in_=xt[xi][:], func=AF.Square,
                                                   accum_out=st[si][:, 0:1]))
                    ACT.run([], [st_b[si]],
                            lambda h: h.activation(out=st[si][:, 1:2], in_=st[si][:, 0:1], func=AF.Sqrt,
                                                   scale=1.0 / D, bias=EPS))
                    DVE.run([], [st_b[si]], lambda h: h.reciprocal(out=st[si][:, 2:3], in_=st[si][:, 1:2]))
                    DVE.run([xt_b[xi], st_b[si]], [hb_b[hi]],
                            lambda h: h.tensor_scalar(out=hb[hi][:], in0=xt[xi][:], scalar1=st[si][:, 2:3],
                                                      scalar2=None, op0=ALU.mult))
                    if t + NXS < NTT:
                        load_x(t + NXS)

                def stage_p(tt):
                    t = 4 * T + tt
                    hi = t % 2
                    for k in range(8):
                        PE.run([hb_b[hi], idb_b], [tp_b[hi]],
                               lambda h: h.transpose(out=tp[hi][:, k * 128:(k + 1) * 128],
                                                     in_=hb[hi][:, k * 128:(k + 1) * 128], identity=idb_t[:]),
                               mark=(k == 7))
                    ACT.run([tp_b[hi]], [hT_b[Ts]],
                            lambda h: h.activation(out=hT[Ts][:, :, tt * 128:(tt + 1) * 128],
                                                   in_=tp[hi][:].rearrange("p (k t) -> p k t", k=8), func=AF.Copy))

                for (kind, tt) in (("e", 0), ("e", 1), ("p", 0), ("e", 2), ("p", 1), ("e", 3), ("p", 2), ("p", 3)):
                    if kind == "e":
                        stage_e(tt)
                    else:
                        stage_p(tt)
                    yield

            def back(T, gen):
                Ts = T % 2

                def mm_fm(col0):
                    i = nxt("mm", NMM)
                    for k in range(8):
                        PE.run([Wp_b, hT_b[Ts]], [mm_b[i]],
                               lambda h: h.matmul(mm[i][:, :], lhsT=Wp[:, k, col0:col0 + 128], rhs=hT[Ts][:, k, :],
                                                  start=(k == 0), stop=(k == 7)),
                               mark=(k == 7))
                    return i

                for c in range(13):
                    o = nxt("o", NO)
                    if c <= 4:
                        rc = 2304 + c * 128
                        bi = 13 + c
                        i0 = mm_fm(c * 128)
                        i1 = mm_fm(rc)
                        r = nxt("rope", 2)
                        DVE.run([mm_b[i0], cs_b[Ts], cst_b], [t1_b[r]],
                                lambda h: h.scalar_tensor_tensor(
                                    out=t1[r][:], in0=mm[i0][:, :], scalar=bfm_t[:, c:c + 1], in1=cst[Ts][:, 0, :],
                                    op0=ALU.add, op1=ALU.mult))
                        DVE.run([mm_b[i1], cs_b[Ts], cst_b], [t2_b[r]],
                                lambda h: h.scalar_tensor_tensor(
                                    out=t2[r][:], in0=mm[i1][:, :], scalar=bfm_t[:, bi:bi + 1], in1=cst[Ts][:, 1, :],
                                    op0=ALU.add, op1=ALU.mult))
                        DVE.run([t1_b[r], t2_b[r]], [ob_b[o]],
                                lambda h: h.tensor_tensor(out=ob[o][:], in0=t1[r][:], in1=t2[r][:], op=ALU.add))
                    elif c <= 8:
                        i0 = mm_fm(c * 128)
                        DVE.run([mm_b[i0], cst_b], [ob_b[o]],
                                lambda h: h.tensor_scalar(out=ob[o][:], in0=mm[i0][:, :], scalar1=bfm_t[:, c:c + 1],
                                                          scalar2=0.125, op0=ALU.add, op1=ALU.mult))
                    else:
                        i0 = mm_fm(c * 128)
                        ACT.run([mm_b[i0], cst_b], [ob_b[o]],
                                lambda h: h.activation(out=ob[o][:], in_=mm[i0][:, :], func=AF.Identity,
                                                       bias=bfm_t[:, c:c + 1], scale=1.0))
                    cx.dma(qkt[c, :, T * 512:(T + 1) * 512], ob[o][:], [ob_b[o]], [], ob_s[o])
                    if c in (2, 6, 10):
                        wjob_step()
                    if gen is not None:
                        next(gen, None)
                for tt in range(4):
                    t = 4 * T + tt
                    vi = t % 2
                    ia = nxt("mm", NMM)
                    ib = nxt("mm", NMM)
                    for k in range(8):
                        PE.run([Wp_b, hT_b[Ts]], [mm_b[ia]],
                               lambda h: h.matmul(mm[ia][:, :], lhsT=hT[Ts][:, k, tt * 128:(tt + 1) * 128],
                                                  rhs=Wp[:, k, 1664:2176], start=(k == 0), stop=(k == 7)),
                               mark=(k == 7))
                    for k in range(8):
                        PE.run([Wp_b, hT_b[Ts]], [mm_b[ib]],
                               lambda h: h.matmul(mm[ib][:, 0:128], lhsT=hT[Ts][:, k, tt * 128:(tt + 1) * 128],
                                                  rhs=Wp[:, k, 2176:2304], start=(k == 0), stop=(k == 7)),
                               mark=(k == 7))
                    DVE.run([mm_b[ia], cst_b], [vp_b[vi]],
                            lambda h: h.tensor_tensor(
                                out=vp[vi][:, 0:8, 0:64], in0=mm[ia][:, :].rearrange("p (h d) -> p h d", h=8),
                                in1=bv_t[:, 0:512].rearrange("p (h d) -> p h d", h=8), op=ALU.add))
                    DVE.run([mm_b[ib], cst_b], [vp_b[vi]],
                            lambda h: h.tensor_tensor(
                                out=vp[vi][:, 8:10, 0:64], in0=mm[ib][:, 0:128].rearrange("p (h d) -> p h d", h=2),
                                in1=bv_t[:, 512:640].rearrange("p (h d) -> p h d", h=2), op=ALU.add))
                    cx.dma(vsc[t * 128:(t + 1) * 128, :], vp[vi][:].rearrange("p h d -> p (h d)"),
                           [vp_b[vi]], [], vp_s[vi])

            wst = [sb("wcs%d" % i, [128, DFF], F32) for i in range(2)]
            wst_b = [Buf() for _ in range(2)]
            wst_s = [cx.dsem() for _ in range(2)]
            wbo = [sb("wcb%d" % i, [128, DFF], BF16) for i in range(2)]
            wbo_b = [Buf() for _ in range(2)]
            wbo_s = [cx.dsem() for _ in range(2)]
            wjobs = []
            for k in range(8):
                wjobs.append((w_o_p[k * 128:(k + 1) * 128, :], D, wbf_o[k * 128:(k + 1) * 128, :], gout_a, k))
            for k in range(8):
                wjobs.append((w_gate[k * 128:(k + 1) * 128, :], DFF, wbf_g[k * 128:(k + 1) * 128, :], gffn_a, k))
            for k in range(8):
                wjobs.append((w_up[k * 128:(k + 1) * 128, :], DFF, wbf_u[k * 128:(k + 1) * 128, :], gffn_a, k))
            for f in range(22):
                wjobs.append((w_down[f * 128:(f + 1) * 128, :], D, wbf_d[f * 128:(f + 1) * 128, :], None, 0))
            wj = [0]

            def wjob_load(ji):
                if ji >= len(wjobs):
                    return
                (src, ncol, dst, gt, k) = wjobs[ji]
                i = ji % 2
                cx.dma(wst[i][:, 0:ncol], src, [], [wst_b[i]], wst_s[i])

            def wjob_step():
                ji = wj[0]
                if ji >= len(wjobs) or "C" not in phases:
                    return
                wj[0] += 1
                if ji == 0:
                    wjob_load(0)
                wjob_load(ji + 1)
                (src, ncol, dst, gt, k) = wjobs[ji]
                i = ji % 2
                hcol = ncol // 2
                if gt is not None:
                    DVE.run([wst_b[i], cst_b], [wbo_b[i]],
                            lambda h: h.tensor_scalar(out=wbo[i][:, 0:hcol], in0=wst[i][:, 0:hcol],
                                                      scalar1=gt[:, k:k + 1], scalar2=None, op0=ALU.mult))
                    ACT.run([wst_b[i], cst_b], [wbo_b[i]],
                            lambda h: h.activation(out=wbo[i][:, hcol:ncol], in_=wst[i][:, hcol:ncol], func=AF.Copy,
                                                   scale=gt[:, k:k + 1]))
                else:
                    DVE.run([wst_b[i]], [wbo_b[i]],
                            lambda h: h.tensor_copy(out=wbo[i][:, 0:hcol], in_=wst[i][:, 0:hcol]))
                    ACT.run([wst_b[i]], [wbo_b[i]],
                            lambda h: h.activation(out=wbo[i][:, hcol:ncol], in_=wst[i][:, hcol:ncol], func=AF.Copy))
                cx.dma(dst, wbo[i][:, 0:ncol], [wbo_b[i]], [], wbo_s[i])

            for _ in front(0):
                pass
            for T in range(NT):
                gen = front(T + 1) if T + 1 < NT else None
                back(T, gen)
                if gen is not None:
                    for _ in gen:
                        pass
            while wj[0] < len(wjobs) and "C" in phases:
                wjob_step()
            cx.barrier()

    def phase_b():
        with contextlib.ExitStack() as ph:
            def sb(name, shape, dt):
                return ph.enter_context(nc.sbuf_tensor(name, list(shape), dt))

            def ps(name, shape, dt=F32):
                return ph.enter_context(nc.psum_tensor(name, list(shape), dt))

            idf_t = sb("b_idf", [128, 128], F32)
            idb_t = sb("b_idb", [128, 128], BF16)
            masks = sb("masks", [128, 12, 256], F32)
            masks_b = Buf("masks")
            masks_s = cx.dsem()
            cst_b = Buf("bconst")
            dsc = cx.dsem()
            cx.dma(idf_t[:], identf[:, :], [], [cst_b], dsc)
            DVE.run([cst_b], [cst_b], lambda h: h.tensor_copy(out=idb_t[:], in_=idf_t[:]))
            qt = [sb("qt%d" % i, [128, 2, 2048], BF16) for i in range(2)]
            qt_b = [Buf() for _ in range(2)]
            qt_s = [cx.dsem() for _ in range(2)]
            qd = [[sb("qd%d_%d" % (j, i), [128, 2, 2048], BF16) for i in range(2)] for j in range(2)]
            qd_b = [[Buf() for _ in range(2)] for _ in range(2)]
            kt = [sb("kt%d" % i, [128, 2, 2048], BF16) for i in range(3)]
            kt_b = [Buf() for _ in range(3)]
            kt_s = [cx.dsem() for _ in range(3)]
            vt = [[sb("vt%d_%d" % (br, i), [128, 16, 260], BF16) for i in range(3)] for br in range(3)]
            vt_b = [[Buf() for _ in range(3)] for _ in range(3)]
            vt_s = [[cx.dsem() for _ in range(3)] for _ in range(3)]
            NPT = 8
            pT = [sb("pT%d" % i, [128, 512], BF16) for i in range(NPT)]
            pT_b = [Buf() for _ in range(NPT)]
            NSS = 6
            ssb = [sb("ssb%d" % i, [128, 512], F32) for i in range(NSS)]
            ssb_b = [Buf() for _ in range(NSS)]
            accsb = [sb("accsb%d" % i, [128, 2048], F32) for i in range(1)]
            accsb_b = [Buf() for _ in range(1)]
            usb = [sb("usb%d" % i, [128, 16, 4, 65], F32) for i in range(1)]
            usb_b = [Buf() for _ in range(1)]
            usb_s = [cx.dsem() for _ in range(1)]
            acc = [ps("acc%d" % i, [128, 512], F32) for i in range(4)]
            acc_b = [Buf() for _ in range(4)]
            sps = [ps("sps%d" % i, [128, 512], F32) for i in range(2)]
            sps_b = [Buf() for _ in range(2)]
            tpu_full = [ps("tpu%d" % i, [128, 512], F32) for i in range(2)]
            tpu = [tf[:, 0:260].rearrange("p (i d) -> p i d", i=4) for tf in tpu_full]
            tpu_b = [Buf() for _ in range(2)]

            cnt = dict(q=0, s=0, p=0, a=0, tp=0, ss=0)

            def nxt(key, n):
                i = cnt[key] % n
                cnt[key] += 1
                return i

            passes = []
            for a in range(2):
                passes.append(dict(grp="A", qch=[2 * a, 2 * a + 1], kch=[4], vc0=0, nvh=2, brs=[0], scale=0.125,
                                   uslot=a))
            for b in range(2):
                passes.append(dict(grp="B", qch=[5 + 2 * b, 6 + 2 * b], kch=[9 + 2 * b, 10 + 2 * b],
                                   vc0=(2 + 4 * b) * 65, nvh=4, brs=[0, 1, 2], scale=1.0, uslot=2 + b))

            def unit_off(dl, u):
                if dl == 1:
                    return u * 128
                if dl == 4:
                    return (u // 4) * 512 + (u % 4)
                return u

            def load_span(pz, s, g):
                par = g % 3
                qs = g % 2
                base = s * 2048
                C = pz["nvh"] * 65
                for i, qc in enumerate(pz["qch"]):
                    cx.dma(qt[qs][:, i, :], qkt[qc, :, base:base + 2048], [], [qt_b[qs]], qt_s[qs])
                for i, kc in enumerate(pz["kch"]):
                    cx.dma(kt[par][:, i, :], qkt[kc, :, base:base + 2048], [], [kt_b[par]], kt_s[par])
                for br in pz["brs"]:
                    dl = DILS[br]
                    src = vsc[base:base + 2048, pz["vc0"]:pz["vc0"] + C]
                    dst = vt[br][par]
                    if dl == 1:
                        for hh in range(2):
                            cx.dma(dst[:, 8 * hh:8 * hh + 8, 0:C],
                                   src[hh * 1024:(hh + 1) * 1024, :].rearrange("(u j) c -> j u c", j=128),
                                   [], [vt_b[br][par]], vt_s[br][par])
                    elif dl == 4:
                        for b4 in range(4):
                            cx.dma(dst[:, 4 * b4:4 * b4 + 4, 0:C],
                                   src[b4 * 512:(b4 + 1) * 512, :].rearrange("(j r) c -> j r c", r=4),
                                   [], [vt_b[br][par]], vt_s[br][par])
                    else:
                        s3 = src.rearrange("(j r) c -> j r c", r=16)
                        for hh in range(2):
                            cx.dma(dst[:, 8 * hh:8 * hh + 8, 0:C], s3[:, 8 * hh:8 * hh + 8, :], [],
                                   [vt_b[br][par]], vt_s[br][par])

            def destride(pz, g):
                qs = g % 2
                if pz["grp"] == "B":
                    for i in range(2):
                        ACT.run([qt_b[qs]], [qd_b[0][qs]],
                                lambda h: h.activation(
                                    out=qd[0][qs][:, i, :].rearrange("p (b r j) -> p b r j", b=4, r=4),
                                    in_=qt[qs][:, i, :].rearrange("p (b j r) -> p b r j", b=4, r=4), func=AF.Copy))
                        ACT.run([qt_b[qs]], [qd_b[1][qs]],
                                lambda h: h.activation(
                                    out=qd[1][qs][:, i, :].rearrange("p (r j) -> p r j", r=16),
                                    in_=qt[qs][:, i, :].rearrange("p (j r) -> p r j", r=16), func=AF.Copy))

            def load_masks(pz):
                if pz["grp"] == "A":
                    cx.dma(masks[:, 0:1, :], maskb[:, 0:1, :], [], [masks_b], masks_s)
                else:
                    b = pz["uslot"] - 2
                    for br in range(3):
                        m0 = 1 + br * 8 + 4 * b
                        cx.dma(masks[:, br * 4:br * 4 + 4, :], maskb[:, m0:m0 + 4, :], [], [masks_b], masks_s)

            def job_info(pz, s, g, dl, units):
                par = g % 3
                ppar_ = (g - 1) % 3
                info = []
                for u in units:
                    if dl == 1:
                        pu = u - 1 if u > 0 else 15
                        ppar = par if u > 0 else ppar_
                        pex = (u > 0) or (s > 0)
                    elif dl == 4:
                        pu = u - 4 if u >= 4 else u + 12
                        ppar = par if u >= 4 else ppar_
                        pex = (u >= 4) or (s > 0)
                    else:
                        pu = u
                        ppar = ppar_
                        pex = s > 0
                    info.append((u, pu, ppar, pex))
                return info

            def emit_S2(step):
                (pz, s, g, c, H, br, item) = step["key"]
                par = g % 3
                qs = g % 2
                dl = DILS[br]
                units = (2 * item, 2 * item + 1) if dl != 16 else tuple(4 * item + i for i in range(4))
                if dl != 16:
                    units = tuple(8 * H + u for u in units)
                info = job_info(pz, s, g, dl, units)
                step["info"] = info
                step["pi"] = []
                mms = [[], []]
                for e in range(2):
                    hi = 2 * c + e
                    pb = e * 64
                    qci = c
                    kci = 0 if pz["grp"] == "A" else c
                    for ui, (u, pu, ppar, pex) in enumerate(info):
                        off = unit_off(dl, u)
                        kcols_d = slice(off, off + dl * 127 + 1, dl)
                        if dl == 1:
                            qsrc, qbuf, qcols, w = qt[qs], qt_b[qs], slice(off, off + 128), 128
                        elif dl == 4:
                            qsrc, qbuf, qcols, w = qd[0][qs], qd_b[0][qs], slice(u * 128, u * 128 + 128), 128
                        else:
                            qsrc, qbuf, w = qd[1][qs], qd_b[1][qs], 64
                            qcols = slice(u * 128 + 64 * H, u * 128 + 64 * H + 64)
                        if pex:
                            poff = unit_off(dl, pu)
                            kp, kcols_p = ppar, slice(poff, poff + dl * 127 + 1, dl)
                        else:
                            kp, kcols_p = par, kcols_d
                        base = ui * 2 * w
                        mms[e].append((kt_b[kp], qbuf, slice(base, base + w), kt[kp][pb:pb + 64, kci, kcols_p],
                                       qsrc[pb:pb + 64, qci, qcols]))
                        mms[e].append((kt_b[par], qbuf, slice(base + w, base + 2 * w),
                                       kt[par][pb:pb + 64, kci, kcols_d], qsrc[pb:pb + 64, qci, qcols]))
                n = len(mms[0])
                for i in range(n):
                    for e in range(2):
                        (kb, qb_, ocols, lhsT, rhs) = mms[e][i]
                        PE.run([kb, qb_], [sps_b[e]],
                               lambda h: h.matmul(sps[e][:, ocols], lhsT=lhsT, rhs=rhs, start=True, stop=True),
                               mark=(i == n - 1))
                for e in range(2):
                    hi = 2 * c + e
                    mi = 0 if pz["grp"] == "A" else br * 4 + hi
                    ss_i = nxt("ss", NSS)
                    pi_ = nxt("p", NPT)
                    step["pi"].append(pi_)
                    if dl != 16:
                        DVE.run([sps_b[e], masks_b], [ssb_b[ss_i]],
                                lambda h: h.tensor_tensor(
                                    out=ssb[ss_i][:].rearrange("p (u c) -> p u c", u=2),
                                    in0=sps[e][:, :].rearrange("p (u c) -> p u c", u=2),
                                    in1=masks[:, mi:mi + 1, :].broadcast_to([128, 2, 256]), op=ALU.add))
                    else:
                        msl = masks[:, mi:mi + 1, :].rearrange("p o (a q) -> p o a q", a=2)[:, :, :,
                                                                                           64 * H:64 * H + 64]
                        DVE.run([sps_b[e], masks_b], [ssb_b[ss_i]],
                                lambda h: h.tensor_tensor(
                                    out=ssb[ss_i][:].rearrange("p (u a q) -> p u a q", u=4, a=2),
                                    in0=sps[e][:, :].rearrange("p (u a q) -> p u a q", u=4, a=2),
                                    in1=msl.broadcast_to([128, 4, 2, 64]), op=ALU.add))
                    ACT.run([ssb_b[ss_i]], [pT_b[pi_]],
                            lambda h: h.activation(out=pT[pi_][:], in_=ssb[ss_i][:], func=AF.Exp, scale=pz["scale"]))

            def emit_PV2(step, first):
                (pz, s, g, c, H, br, item) = step["key"]
                par = g % 3
                dl = DILS[br]
                for e in range(2):
                    hi = 2 * c + e
                    vh = e if pz["grp"] == "A" else hi
                    pi_ = step["pi"][e]
                    pvs = []
                    for ui, (u, pu, ppar, pex) in enumerate(step["info"]):
                        for part in (0, 1):
                            if part == 0 and not pex:
                                continue
                            vslot = ppar if part == 0 else par
                            vu = pu if part == 0 else u
                            if dl == 1:
                                lo = (u - 8 * H) * 128
                                pvs.append((lo // 512, slice(lo % 512, lo % 512 + 128), vslot, vu,
                                            ui * 256 + part * 128, 128))
                            elif dl == 4:
                                r = u % 4
                                pvs.append((u // 4 - 2 * H, slice(r, r + 4 * 127 + 1, 4), vslot, vu,
                                            ui * 256 + part * 128, 128))
                            else:
                                for q2 in range(2):
                                    pvs.append((q2, slice(u, u + 16 * 31 + 1, 16), vslot, vu,
                                                ui * 128 + part * 64 + 32 * q2, 32))
                    for ii, (bank, cols, vslot, vu, p0, n) in enumerate(pvs):
                        st_ = first[e][bank]
                        first[e][bank] = False
                        PE.run([vt_b[br][vslot], pT_b[pi_]], [acc_b[2 * e + bank]],
                               lambda h: h.matmul(acc[2 * e + bank][0:65, cols],
                                                  lhsT=vt[br][vslot][:, vu, vh * 65:(vh + 1) * 65],
                                                  rhs=pT[pi_][:, p0:p0 + n], start=st_, stop=False,
                                                  skip_group_check=True),
                               mark=(ii == len(pvs) - 1))

            def emit_evac2():
                for e in range(2):
                    for bank in range(2):
                        dst = accsb[0][0:65, e * 1024 + bank * 512:e * 1024 + (bank + 1) * 512]
                        if False:
                            ACT.run([acc_b[2 * e + bank]], [accsb_b[0]],
                                    lambda h: h.activation(out=dst, in_=acc[2 * e + bank][0:65, :], func=AF.Copy))
                        else:
                            DVE.run([acc_b[2 * e + bank]], [accsb_b[0]],
                                    lambda h: h.tensor_copy(out=dst, in_=acc[2 * e + bank][0:65, :]))

            def emit_tr2(c, H):
                for e in range(2):
                    hi = 2 * c + e
                    for gq in range(2):
                        ti = nxt("tp", 2)
                        for i in range(4):
                            tl = 4 * gq + i
                            PE.run([accsb_b[0], cst_b], [tpu_b[ti]],
                                   lambda h: h.transpose(
                                       out=tpu[ti][:, i, :],
                                       in_=accsb[0][0:65, e * 1024 + tl * 128:e * 1024 + (tl + 1) * 128],
                                       identity=idf_t[0:65, 0:65]), mark=(i == 3))
                        DVE.run([tpu_b[ti]], [usb_b[0]],
                                lambda h: h.tensor_copy(
                                    out=usb[0][:, 8 * H + 4 * gq:8 * H + 4 * gq + 4, hi, :], in_=tpu[ti]))

            spans = [(pz, s) for pz in passes for s in range(NSP)]
            load_masks(spans[0][0])
            load_span(spans[0][0], spans[0][1], 0)
            destride(spans[0][0], 0)
            for g, (pz, s) in enumerate(spans):
                if True:
                    base = s * 2048
                    if s == 0 and g > 0:
                        load_masks(pz)
                    nxt_span = spans[g + 1] if g + 1 < len(spans) else None
                    if nxt_span is not None:
                        load_span(nxt_span[0], nxt_span[1], g + 1)
                    groups = [(c, H) for c in range(2) for H in range(2)]
                    steps = []
                    for (c, H) in groups:
                        for br in pz["brs"]:
                            for item in range(4):
                                steps.append(dict(key=(pz, s, g, c, H, br, item), grp=(c, H)))
                    first = [[True, True], [True, True]]
                    pend_tr = None
                    LA = 3
                    nj = len(steps)
                    for idx in range(nj + LA):
                        if idx == nj // 4 and nxt_span is not None:
                            destride(nxt_span[0], g + 1)
                        if idx < nj:
                            emit_S2(steps[idx])
                        if idx >= LA:
                            pj = steps[idx - LA]
                            emit_PV2(pj, first)
                            if pend_tr is not None:
                                emit_tr2(*pend_tr)
                                pend_tr = None
                            if idx - LA + 1 >= nj or steps[idx - LA + 1]["grp"] != pj["grp"]:
                                emit_evac2()
                                pend_tr = pj["grp"]
                                first = [[True, True], [True, True]]
                    if pend_tr is not None:
                        emit_tr2(*pend_tr)
                    c0 = pz["uslot"] * 260
                    for hh in range(2):
                        cx.dma(usc[base + hh * 1024:base + (hh + 1) * 1024, c0:c0 + 260].rearrange(
                                   "(u j) c -> j u c", j=128),
                               usb[0][:, 8 * hh:8 * hh + 8, :, :].rearrange("p u h d -> p u (h d)"),
                               [usb_b[0]], [], usb_s[0])
            cx.barrier()

    def phase_c():
        with contextlib.ExitStack() as ph:
            def sb(name, shape, dt):
                return ph.enter_context(nc.sbuf_tensor(name, list(shape), dt))

            def ps(name, shape, dt=F32):
                return ph.enter_context(nc.psum_tensor(name, list(shape), dt))

            Wo = sb("Wo", [128, 8, D], BF16)
            Wg = sb("Wg", [128, 8, DFF], BF16)
            Wu = sb("Wu", [128, 8, DFF], BF16)
            Wd = sb("Wd", [128, 22, D], BF16)
            Wo_b, Wg_b, Wu_b, Wd_b = Buf("Wo"), Buf("Wg"), Buf("Wu"), Buf("Wd")
            idf_t = sb("c_idf", [128, 128], F32)
            idb_t = sb("c_idb", [128, 128], BF16)
            gfin_t = sb("gfin_t", [128, D], F32)
            ex16 = sb("ex16", [128, 16], F32)
            cst_b = Buf("cconst")
            dsc = cx.dsem()
            cx.dma(idf_t[:], identf[:, :], [], [cst_b], dsc)
            cx.dma(gfin_t[:], gfin_bc[:, :], [], [cst_b], dsc)
            cx.dma(ex16[:, 0:8], sinks_bc[:, :], [], [cst_b], dsc)
            DVE.run([cst_b], [cst_b], lambda h: h.tensor_copy(out=idb_t[:], in_=idf_t[:]))
            ACT.run([cst_b], [cst_b], lambda h: h.activation(out=ex16[:, 0:8], in_=ex16[:, 0:8], func=AF.Exp))
            DVE.run([cst_b], [cst_b], lambda h: h.memset(ex16[:, 8:16], 0.0))
            uin = [sb("uin%d" % i, [128, 16, 65], F32) for i in range(2)]
            uin_b = [Buf() for _ in range(2)]
            uin_s = [cx.dsem() for _ in range(2)]
            NX1 = 4
            x1 = [sb("x1_%d" % i, [128, D], F32) for i in range(NX1)]
            x1_b = [Buf() for _ in range(NX1)]
            x1_s = [cx.dsem() for _ in range(NX1)]

            def load_u(R):
                for tt in range(2):
                    t = 2 * R + tt
                    ui = t % 2
                    cx.dma(uin[ui][:].rearrange("p h d -> p (h d)"), usc[t * 128:(t + 1) * 128, :], [],
                           [uin_b[ui]], uin_s[ui])

            def load_x(R):
                for tt in range(2):
                    t = 2 * R + tt
                    xi = t % NX1
                    cx.dma(x1[xi][:], x[t * 128:(t + 1) * 128, :], [], [x1_b[xi]], x1_s[xi])

            ws = [cx.dsem() for _ in range(4)]
            load_u(0)
            cx.dma(Wo[:], wbf_o.rearrange("(k p) c -> p k c", p=128), [], [Wo_b], ws[0])
            load_x(0)
            for k0 in range(0, 8, 2):
                cx.dma(Wg[:, k0:k0 + 2, :], wbf_g[k0 * 128:(k0 + 2) * 128, :].rearrange("(k p) c -> p k c", p=128),
                       [], [Wg_b], ws[1])
                cx.dma(Wu[:, k0:k0 + 2, :], wbf_u[k0 * 128:(k0 + 2) * 128, :].rearrange("(k p) c -> p k c", p=128),
                       [], [Wu_b], ws[2])
            for f0 in range(0, 22, 11):
                cx.dma(Wd[:, f0:f0 + 11, :],
                       wbf_d[f0 * 128:(f0 + 11) * 128, :].rearrange("(k p) c -> p k c", p=128), [], [Wd_b], ws[3])
            junk = sb("cjunk", [128, D], mybir.dt.float8e4)
            st = [sb("cst%d" % i, [128, 4], F32) for i in range(8)]
            st_b = [Buf() for _ in range(8)]
            den = sb("den", [128, 32], F32)
            den_b = Buf()
            ot = sb("ot", [128, 16, 64], F32)
            ot_b = Buf()
            mx = sb("mx", [128, D], BF16)
            mx_b = Buf()
            tp = [ps("ctp%d" % i, [128, D], BF16) for i in range(2)]
            tp_b = [Buf() for _ in range(2)]
            mT = sb("mT", [128, 8, 128], BF16)
            mT_b = Buf()
            h2T = [sb("h2T%d" % i, [128, 8, 256], BF16) for i in range(2)]
            h2T_b = [Buf() for _ in range(2)]
            sg = [sb("sg%d" % i, [128, 256], F32) for i in range(2)]
            sg_b = [Buf() for _ in range(2)]
            actT = sb("actT", [128, 22, 256], BF16)
            actT_b = Buf()
            NPM, NPG = 2, 4
            pm = [ps("pm%d" % i, [128, 512], F32) for i in range(NPM)]
            pm_b = [Buf() for _ in range(NPM)]
            pg = [ps("pg%d" % i, [128, 2, 256], F32) for i in range(NPG)]
            pg_b = [Buf() for _ in range(NPG)]
            cnt = dict(pm=0, pg=0, tp=0, st=0, sg=0)

            def nxt(key, n):
                i = cnt[key] % n
                cnt[key] += 1
                return i

            def rms_stats(src_ap, src_b, si, ncols):
                ACT.run([src_b], [st_b[si]],
                        lambda h: h.activation(out=junk[:, 0:ncols], in_=src_ap, func=AF.Square,
                                               accum_out=st[si][:, 0:1], saturate=False))
                ACT.run([], [st_b[si]],
                        lambda h: h.activation(out=st[si][:, 1:2], in_=st[si][:, 0:1], func=AF.Sqrt,
                                               scale=1.0 / ncols, bias=EPS))
                DVE.run([], [st_b[si]], lambda h: h.reciprocal(out=st[si][:, 2:3], in_=st[si][:, 1:2]))

            def front(R):
                hs = R % 2
                for tt in range(2):
                    t = 2 * R + tt
                    ui = t % 2
                    xi = t % NX1
                    DVE.run([uin_b[ui], cst_b], [den_b],
                            lambda h: h.tensor_tensor(out=den[:, 0:16], in0=uin[ui][:, :, 64], in1=ex16[:],
                                                      op=ALU.add))
                    DVE.run([], [den_b], lambda h: h.reciprocal(out=den[:, 16:32], in_=den[:, 0:16]))
                    DVE.run([uin_b[ui], den_b], [ot_b],
                            lambda h: h.tensor_tensor(
                                out=ot[:], in0=uin[ui][:, :, 0:64],
                                in1=den[:, 16:32].unsqueeze(2).broadcast_to([128, 16, 64]), op=ALU.mult))
                    yield
                    sA = nxt("st", 8)
                    rms_stats(ot[:, 0:8, :].rearrange("p h d -> p (h d)"), ot_b, sA, 512)
                    sB = nxt("st", 8)
                    rms_stats(ot[:, 8:16, :].rearrange("p h d -> p (h d)"), ot_b, sB, 512)
                    ACT.run([ot_b, st_b[sA]], [mx_b],
                            lambda h: h.activation(out=mx[:, 0:512], in_=ot[:, 0:8, :].rearrange("p h d -> p (h d)"),
                                                   func=AF.Copy, scale=st[sA][:, 2:3]))
                    DVE.run([ot_b, st_b[sB]], [mx_b],
                            lambda h: h.tensor_scalar(out=mx[:, 512:1024],
                                                      in0=ot[:, 8:16, :].rearrange("p h d -> p (h d)"),
                                                      scalar1=st[sB][:, 2:3], scalar2=None, op0=ALU.mult))
                    yield
                    yield
                    ti = nxt("tp", 2)
                    for k in range(8):
                        PE.run([mx_b, cst_b], [tp_b[ti]],
                               lambda h: h.transpose(out=tp[ti][:, k * 128:(k + 1) * 128],
                                                     in_=mx[:, k * 128:(k + 1) * 128], identity=idb_t[:]),
                               mark=(k == 7))
                    ACT.run([tp_b[ti]], [mT_b],
                            lambda h: h.activation(out=mT[:], in_=tp[ti][:].rearrange("p (k t) -> p k t", k=8),
                                                   func=AF.Copy))
                    yield
                    yield
                    for half in range(2):
                        pi_ = nxt("pm", NPM)
                        for k in range(8):
                            PE.run([mT_b, Wo_b], [pm_b[pi_]],
                                   lambda h: h.matmul(pm[pi_][:, :], lhsT=mT[:, k, :],
                                                      rhs=Wo[:, k, half * 512:(half + 1) * 512],
                                                      start=(k == 0), stop=(k == 7)), mark=(k == 7))
                        DVE.run([pm_b[pi_]], [x1_b[xi]],
                                lambda h: h.tensor_tensor(out=x1[xi][:, half * 512:(half + 1) * 512],
                                                          in0=pm[pi_][:, :],
                                                          in1=x1[xi][:, half * 512:(half + 1) * 512], op=ALU.add))
                    s2 = nxt("st", 8)
                    rms_stats(x1[xi][:], x1_b[xi], s2, D)
                    DVE.run([x1_b[xi], st_b[s2]], [mx_b],
                            lambda h: h.tensor_scalar(out=mx[:], in0=x1[xi][:], scalar1=st[s2][:, 2:3],
                                                      scalar2=None, op0=ALU.mult))
                    yield
                    yield
                    yield
                    ti = nxt("tp", 2)
                    for k in range(8):
                        PE.run([mx_b, cst_b], [tp_b[ti]],
                               lambda h: h.transpose(out=tp[ti][:, k * 128:(k + 1) * 128],
                                                     in_=mx[:, k * 128:(k + 1) * 128], identity=idb_t[:]),
                               mark=(k == 7))
                    ACT.run([tp_b[ti]], [h2T_b[hs]],
                            lambda h: h.activation(out=h2T[hs][:, :, tt * 128:(tt + 1) * 128],
                                                   in_=tp[ti][:].rearrange("p (k t) -> p k t", k=8), func=AF.Copy))
                    yield

            def ffn(R, gen):
                hs = R % 2
                for f in range(22):
                    gi = nxt("pg", NPG)
                    for w_i, (W, Wb) in enumerate(((Wg, Wg_b), (Wu, Wu_b))):
                        for k in range(8):
                            PE.run([h2T_b[hs], Wb], [pg_b[gi]],
                                   lambda h: h.matmul(pg[gi][:, w_i, :], lhsT=W[:, k, f * 128:(f + 1) * 128],
                                                      rhs=h2T[hs][:, k, :], start=(k == 0), stop=(k == 7)),
                                   mark=(k == 7 and w_i == 1))
                    gs = nxt("sg", 2)
                    ACT.run([pg_b[gi]], [sg_b[gs]],
                            lambda h: h.activation(out=sg[gs][:], in_=pg[gi][:, 0, :], func=AF.Silu))
                    DVE.run([pg_b[gi], sg_b[gs]], [actT_b],
                            lambda h: h.tensor_tensor(out=actT[:, f, :], in0=pg[gi][:, 1, :], in1=sg[gs][:],
                                                      op=ALU.mult))
                    if gen is not None and f >= 3:
                        next(gen, None)
                if gen is not None:
                    for _ in gen:
                        pass
                for tt in range(2):
                    t = 2 * R + tt
                    xi = t % NX1
                    for half in range(2):
                        pi_ = nxt("pm", NPM)
                        for f in range(22):
                            PE.run([actT_b, Wd_b], [pm_b[pi_]],
                                   lambda h: h.matmul(pm[pi_][:, :], lhsT=actT[:, f, tt * 128:(tt + 1) * 128],
                                                      rhs=Wd[:, f, half * 512:(half + 1) * 512],
                                                      start=(f == 0), stop=(f == 21)), mark=(f == 21))
                        DVE.run([pm_b[pi_]], [x1_b[xi]],
                                lambda h: h.tensor_tensor(out=x1[xi][:, half * 512:(half + 1) * 512],
                                                          in0=pm[pi_][:, :],
                                                          in1=x1[xi][:, half * 512:(half + 1) * 512], op=ALU.add))
                    s3 = nxt("st", 8)
                    rms_stats(x1[xi][:], x1_b[xi], s3, D)
                    DVE.run([st_b[s3], cst_b], [x1_b[xi]],
                            lambda h: h.scalar_tensor_tensor(out=x1[xi][:], in0=x1[xi][:], scalar=st[s3][:, 2:3],
                                                             in1=gfin_t[:], op0=ALU.mult, op1=ALU.mult))
                    cx.dma(y[t * 128:(t + 1) * 128, :], x1[xi][:], [x1_b[xi]], [], x1_s[xi])

            if NR > 1:
                load_x(1)
            for _ in front(0):
                pass
            if NR > 1:
                load_u(1)
            for R in range(NR):
                gen = front(R + 1) if R + 1 < NR else None
                ffn(R, gen)
                if R + 2 < NR:
                    load_u(R + 2)
                    load_x(R + 2)
            cx.barrier()

    if "A" in phases:
        phase_a()
    if "B" in phases:
        phase_b()
    if "C" in phases:
        phase_c()
    return nc


def _colperm():
    cols = []
    for c in range(4):
        for hd in (c, 4 + c):
            cols += list(range(hd * 64, hd * 64 + 64))
    cols += list(range(512, 640))
    cols += list(range(768, 1280))
    cols += list(range(1280, 1792))
    cols += list(range(640, 768))
    cols += list(range(1792, 2304))
    for c in range(4):
        for hd in (c, 4 + c):
            b = hd * 64
            cols += list(range(b + 32, b + 64)) + list(range(b, b + 32))
    for kv in range(2):
        b = 512 + kv * 64
        cols += list(range(b + 32, b + 64)) + list(range(b, b + 32))
    return np.array(cols, dtype=np.int64)


def _t5_bucket(dist):
    dist = np.asarray(dist, dtype=np.int64)
    max_exact = 16
    df = np.maximum(dist, 1).astype(np.float32)
    large = max_exact + (np.log(df / np.float32(max_exact)) / np.float32(np.log(2048 / 16))
                         * np.float32(32 - max_exact)).astype(np.int32)
    large = np.minimum(large, 31)
    return np.where(dist < max_exact, dist, large)


def _host_prep(inputs, S):
    f32 = np.float32
    cp = _colperm()
    w_in = np.asarray(inputs["w_in"], dtype=f32)[0]
    b_in = np.asarray(inputs["b_in"], dtype=f32)[0]
    w_in_p = np.ascontiguousarray(w_in[:, cp])
    b_p = b_in[cp]
    chunks = list(range(13)) + [18 + i for i in range(5)]
    b_fm = np.stack([b_p[c * 128:(c + 1) * 128] for c in chunks], axis=1).astype(f32)
    b_v = np.ascontiguousarray(np.broadcast_to(b_p[1664:2304][None, :], (128, 640))).astype(f32)

    def pm(v):
        return np.ascontiguousarray(np.asarray(v, dtype=f32).reshape(8, 128).T)

    gat = pm(inputs["g_attn"][0])
    gffn = pm(inputs["g_ffn"][0])
    rowperm = []
    for hd in A_ORDER:
        rowperm += list(range(hd * 64, hd * 64 + 64))
    rowperm += list(range(512, 1024))
    rowperm = np.array(rowperm)
    g_out = np.concatenate([np.asarray(inputs["g_out_a"], f32)[0], np.asarray(inputs["g_out_b"], f32)[0]])
    gout = pm(g_out[rowperm])
    w_o_p = np.ascontiguousarray(np.asarray(inputs["w_o"], f32)[0][rowperm, :])
    sinks = np.asarray(inputs["sinks"], f32)[0][A_ORDER]
    sinks_bc = np.ascontiguousarray(np.broadcast_to(sinks[None, :], (128, 8))).astype(f32)
    gfin_bc = np.ascontiguousarray(np.broadcast_to(np.asarray(inputs["g_final"], f32)[None, :], (128, D)))
    half = 32
    inv_freq = (150000.0 ** (-np.arange(half, dtype=np.float64) / half)).astype(f32)
    ang = (np.arange(S, dtype=f32)[:, None] * inv_freq[None, :]).astype(f32)
    cosv = np.cos(ang).astype(f32).T
    sinv = np.sin(ang).astype(f32).T
    cs = np.empty((2, 128, S), f32)
    for p in range(128):
        i = p % 32
        hf = (p % 64) // 32
        cs[0, p] = cosv[i]
        cs[1, p] = -sinv[i] if hf == 0 else sinv[i]
    rel = np.asarray(inputs["rel_table"], f32)
    sidx = np.arange(128)[:, None]
    qidx = np.arange(128)[None, :]
    d_prev = 128 + qidx - sidx
    d_diag = qidx - sidx
    maskb = np.empty((128, 25, 256), f32)
    maskb[:, 0, 0:128] = np.where(d_prev <= 127, 0.0, NEG)
    maskb[:, 0, 128:256] = np.where(d_diag >= 0, 0.0, NEG)
    for br, dl in enumerate(DILS):
        bp = _t5_bucket(np.clip(d_prev, 0, 128) * dl)
        bd = _t5_bucket(np.clip(d_diag, 0, 128) * dl)
        for hd in range(8):
            mi = 1 + br * 8 + hd
            maskb[:, mi, 0:128] = np.where(d_prev <= 128, rel[bp, hd], NEG)
            maskb[:, mi, 128:256] = np.where(d_diag >= 0, rel[bd, hd], NEG)
    shared = dict(
        w_in_p=w_in_p, b_fm=b_fm, b_v=b_v, gat=gat, cs=cs, maskb=maskb, identf=np.eye(128, dtype=f32),
        sinks_bc=sinks_bc, w_o_p=w_o_p, gout=gout, gffn=gffn,
        w_gate=np.ascontiguousarray(np.asarray(inputs["w_gate"], f32)[0]),
        w_up=np.ascontiguousarray(np.asarray(inputs["w_up"], f32)[0]),
        w_down=np.ascontiguousarray(np.asarray(inputs["w_down"], f32)[0]),
        gfin_bc=gfin_bc,
    )
    return shared


_NC_CACHE = {}


def kernel(x, g_attn, w_in, b_in, sinks, rel_table, g_out_a, g_out_b, w_o, g_ffn, w_gate, w_up, w_down,
           g_final):
    inputs = dict(x=x, g_attn=g_attn, w_in=w_in, b_in=b_in, sinks=sinks, rel_table=rel_table, g_out_a=g_out_a,
                  g_out_b=g_out_b, w_o=w_o, g_ffn=g_ffn, w_gate=w_gate, w_up=w_up, w_down=w_down,
                  g_final=g_final)
    x = np.asarray(x, dtype=np.float32)
    B, S, _ = x.shape
    shared = _host_prep(inputs, S)
    if S not in _NC_CACHE:
        _NC_CACHE[S] = build(S)
    nc = _NC_CACHE[S]
    in_maps = []
    for b in range(B):
        m = dict(shared)
        m["x"] = np.ascontiguousarray(x[b])
        in_maps.append(m)
    res = run_bass_kernel_spmd(nc, in_maps, core_ids=list(range(B)))
    out = np.stack([np.asarray(r["y"], dtype=np.float32) for r in res.results], axis=0)
    return out
```
